# Optimizing a Trainium2 kernel written in Bass

```python
import jax
import jax.numpy as jnp
from jax import lax
import numpy as np

D_MODEL = 2048
BATCH = 4
SEQ = 2048
DEPTH = 4

N_MIXERS = 4
HEAD_DIM = 128
N_HEADS = 16
MIX_WIDTH = N_HEADS * HEAD_DIM
N_MEM = 256
MEM_HEADS = 4
MEM_WIDTH = MEM_HEADS * HEAD_DIM
BLOCK = 128
ROPE_THETA = 10000.0
EPS = 1e-6
IDX_HEADS = 16
IDX_DIM = 64
TOPK_MAX = 256
DILATED_PAIRS = ((128, 1), (512, 4), (2048, 16))
N_DIL_GROUPS = len(DILATED_PAIRS)
DIL_HEADS = 6
DIL_WIDTH = DIL_HEADS * HEAD_DIM
Q_LORA = 512
KV_LORA = 512
QK_NOPE = 128
QK_ROPE = 64
V_HEAD = 128

A_SIZES = (MIX_WIDTH, MIX_WIDTH, MIX_WIDTH, N_HEADS, MEM_WIDTH, MIX_WIDTH + MEM_WIDTH)
B_SIZES = (MIX_WIDTH, HEAD_DIM, HEAD_DIM, IDX_HEADS * IDX_DIM, IDX_DIM, IDX_HEADS, MEM_WIDTH, MIX_WIDTH + MEM_WIDTH)
C_SIZES = (N_DIL_GROUPS * DIL_WIDTH, N_DIL_GROUPS * DIL_WIDTH, N_DIL_GROUPS * DIL_WIDTH, MEM_WIDTH, DIL_WIDTH + MEM_WIDTH)
D_SIZES = (Q_LORA, KV_LORA, QK_ROPE, MEM_WIDTH, N_HEADS * V_HEAD + MEM_WIDTH)

F32 = jnp.float32

kernel_name = 'hybrid_fox_dsa_dilated_mla_block'


def rms_norm(x, g):
    x32 = x.astype(F32)
    y = x32 * lax.rsqrt(jnp.mean(x32 * x32, axis=-1, keepdims=True) + EPS)
    return y.astype(x.dtype) * g


def rope(x, pos):
    dh = x.shape[-1]
    half = dh // 2
    inv_freq = jnp.power(ROPE_THETA, -jnp.arange(half, dtype=F32) * 2.0 / dh)
    ang = pos.astype(F32)[:, :, None, None] * inv_freq
    cos, sin = jnp.cos(ang), jnp.sin(ang)
    x32 = x.astype(F32)
    x1, x2 = x32[..., :half], x32[..., half:]
    return jnp.concatenate([x1 * cos - x2 * sin, x2 * cos + x1 * sin], axis=-1).astype(x.dtype)


def split_cols(u, sizes):
    bounds = []
    acc = 0
    for s in sizes[:-1]:
        acc += s
        bounds.append(acc)
    return jnp.split(u, bounds, axis=-1)


def to_blocks(a):
    b, s = a.shape[:2]
    return jnp.moveaxis(a.reshape(b, s // BLOCK, BLOCK, *a.shape[2:]), 1, 0)


def from_blocks(a):
    nb, b, blk = a.shape[:3]
    return jnp.moveaxis(a, 0, 1).reshape(b, nb * blk, *a.shape[3:])


def masked_softmax(logits, mask):
    return jax.nn.softmax(jnp.where(mask, logits, -jnp.inf), axis=-1)


def forgetting_attention(q, k, v, log_f):
    s_len, dh = q.shape[1], q.shape[-1]
    c = jnp.cumsum(log_f, axis=1)
    c_keys = jnp.moveaxis(c, 1, 2)[:, :, None, :]
    kpos = jnp.arange(s_len)
    scale = dh ** -0.5

    def block(args):
        qb, cb, i = args
        qpos = i * BLOCK + jnp.arange(BLOCK)
        s = jnp.einsum('bqhd,bkhd->bhqk', qb, k).astype(F32) * scale
        s = s + jnp.moveaxis(cb, 1, 2)[..., None] - c_keys
        p = masked_softmax(s, kpos[None, :] <= qpos[:, None])
        return jnp.einsum('bhqk,bkhd->bqhd', p.astype(v.dtype), v)

    out = lax.map(block, (to_blocks(q), to_blocks(c), jnp.arange(s_len // BLOCK)))
    return from_blocks(out)


def dsa_attention(q, k, v, q_idx, k_idx, w_idx):
    s_len, dh = q.shape[1], q.shape[-1]
    n_sel = min(TOPK_MAX, s_len // 4)
    scale = dh ** -0.5
    idx_scale = (IDX_DIM ** -0.5) * (IDX_HEADS ** -0.5)
    kpos = jnp.arange(s_len)
    gather = jax.vmap(lambda a, i: a[i])

    def block(args):
        qb, qib, wb, i = args
        qpos = i * BLOCK + jnp.arange(BLOCK)
        rel = jax.nn.relu(jnp.einsum('bqhd,bsd->bqhs', qib, k_idx).astype(F32))
        score = jnp.einsum('bqh,bqhs->bqs', wb.astype(F32), rel) * idx_scale
        score = jnp.where(kpos[None, None, :] <= qpos[None, :, None], score, -jnp.inf)
        _, sel = lax.top_k(score, n_sel)
        valid = sel <= qpos[None, :, None]
        kg = gather(k, sel)
        vg = gather(v, sel)
        s = jnp.einsum('bqhd,bqjd->bhqj', qb, kg).astype(F32) * scale
        p = masked_softmax(s, valid[:, None])
        return jnp.einsum('bhqj,bqjd->bqhd', p.astype(v.dtype), vg)

    out = lax.map(block, (to_blocks(q), to_blocks(q_idx), to_blocks(w_idx), jnp.arange(s_len // BLOCK)))
    return from_blocks(out)


def dilated_attention(q, k, v):
    s_len, dh = q.shape[1], q.shape[-1]
    scale = dh ** -0.5
    k_groups = [k[:, :, g] for g in range(N_DIL_GROUPS)]
    v_groups = [v[:, :, g] for g in range(N_DIL_GROUPS)]

    def block(args):
        qb, i = args
        qpos = i * BLOCK + jnp.arange(BLOCK)
        outs, lses = [], []
        for g, (window, dil) in enumerate(DILATED_PAIRS):
            offs = jnp.arange(window // dil + 1) * dil
            kpos = qpos[:, None] - offs[None, :]
            valid = kpos >= 0
            kidx = jnp.maximum(kpos, 0)
            kg = jnp.take(k_groups[g], kidx, axis=1)
            vg = jnp.take(v_groups[g], kidx, axis=1)
            s = jnp.einsum('bqhd,bqjhd->bhqj', qb[:, :, g], kg).astype(F32) * scale
            s = jnp.where(valid, s, -jnp.inf)
            lse = jax.nn.logsumexp(s, axis=-1, keepdims=True)
            p = jnp.exp(s - lse)
            outs.append(jnp.einsum('bhqj,bqjhd->bqhd', p.astype(v.dtype), vg))
            lses.append(jnp.moveaxis(lse[..., 0], 1, 2))
        alpha = jax.nn.softmax(jnp.stack(lses, axis=-1), axis=-1)
        o = jnp.stack(outs, axis=-1).astype(F32)
        return jnp.einsum('bqhdg,bqhg->bqhd', o, alpha).astype(v.dtype)

    out = lax.map(block, (to_blocks(q), jnp.arange(s_len // BLOCK)))
    return from_blocks(out)


def mla_attention(q_nope, q_rope, k_nope, k_rope, v):
    s_len = q_nope.shape[1]
    scale = (QK_NOPE + QK_ROPE) ** -0.5
    kpos = jnp.arange(s_len)

    def block(args):
        qn, qr, i = args
        qpos = i * BLOCK + jnp.arange(BLOCK)
        s = (jnp.einsum('bqhd,bkhd->bhqk', qn, k_nope)
             + jnp.einsum('bqhr,bkr->bhqk', qr, k_rope)).astype(F32) * scale
        p = masked_softmax(s, kpos[None, :] <= qpos[:, None])
        return jnp.einsum('bhqk,bkhd->bqhd', p.astype(v.dtype), v)

    out = lax.map(block, (to_blocks(q_nope), to_blocks(q_rope), jnp.arange(s_len // BLOCK)))
    return from_blocks(out)


def memory_attention(q_mem, mem_k, mem_v):
    s = jnp.einsum('bqhd,bnhd->bhqn', q_mem, mem_k).astype(F32) * HEAD_DIM ** -0.5
    p = jax.nn.softmax(s, axis=-1).astype(mem_v.dtype)
    return jnp.einsum('bhqn,bnhd->bqhd', p, mem_v)


def forgetting_mixer(h, pos, w_in, forget_bias):
    b, s, _ = h.shape
    q, k, v, f, q_mem, z = split_cols(h @ w_in, A_SIZES)
    log_f = jax.nn.log_sigmoid(f.astype(F32) + forget_bias.astype(F32))
    hs = (b, s, N_HEADS, HEAD_DIM)
    y = forgetting_attention(q.reshape(hs), k.reshape(hs), v.reshape(hs), log_f)
    return y.reshape(b, s, MIX_WIDTH), q_mem, z


def dsa_mixer(h, pos, w_in):
    b, s, _ = h.shape
    q, k, v, q_idx, k_idx, w_idx, q_mem, z = split_cols(h @ w_in, B_SIZES)
    q = rope(q.reshape(b, s, N_HEADS, HEAD_DIM), pos)
    k = rope(k[:, :, None, :], pos)[:, :, 0]
    q_idx = rope(q_idx.reshape(b, s, IDX_HEADS, IDX_DIM), pos)
    k_idx = rope(k_idx[:, :, None, :], pos)[:, :, 0]
    y = dsa_attention(q, k, v, q_idx, k_idx, w_idx)
    return y.reshape(b, s, MIX_WIDTH), q_mem, z


def dilated_mixer(h, pos, w_in):
    b, s, _ = h.shape
    q, k, v, q_mem, z = split_cols(h @ w_in, C_SIZES)
    flat = (b, s, N_DIL_GROUPS * DIL_HEADS, HEAD_DIM)
    grp = (b, s, N_DIL_GROUPS, DIL_HEADS, HEAD_DIM)
    q = rope(q.reshape(flat), pos).reshape(grp)
    k = rope(k.reshape(flat), pos).reshape(grp)
    y = dilated_attention(q, k, v.reshape(grp))
    return y.reshape(b, s, DIL_WIDTH), q_mem, z


def mla_mixer(h, pos, w_in, q_norm, w_uq, kv_norm, w_ukv):
    b, s, _ = h.shape
    c_q, c_kv, k_rope, q_mem, z = split_cols(h @ w_in, D_SIZES)
    qf = (rms_norm(c_q, q_norm) @ w_uq).reshape(b, s, N_HEADS, QK_NOPE + QK_ROPE)
    q_nope, q_rope = qf[..., :QK_NOPE], rope(qf[..., QK_NOPE:], pos)
    kvf = (rms_norm(c_kv, kv_norm) @ w_ukv).reshape(b, s, N_HEADS, QK_NOPE + V_HEAD)
    k_nope, v = kvf[..., :QK_NOPE], kvf[..., QK_NOPE:]
    k_rope = rope(k_rope[:, :, None, :], pos)[:, :, 0]
    y = mla_attention(q_nope, q_rope, k_nope, k_rope, v)
    return y.reshape(b, s, N_HEADS * V_HEAD), q_mem, z


def hybrid_layer(x, mem, norm_g, mem_norm_g, w_mem_kv, w_out, mixer):
    b, s, _ = x.shape
    y, q_mem, z = mixer(rms_norm(x, norm_g))
    kv = (rms_norm(mem, mem_norm_g) @ w_mem_kv).reshape(b, mem.shape[1], 2, MEM_HEADS, HEAD_DIM)
    y_mem = memory_attention(q_mem.reshape(b, s, MEM_HEADS, HEAD_DIM), kv[:, :, 0], kv[:, :, 1])
    gated = jnp.concatenate([y, y_mem.reshape(b, s, MEM_WIDTH)], axis=-1) * jax.nn.silu(z)
    return x + gated @ w_out


def setup_inputs(seed: int = 0) -> dict:
    key = jax.random.key(seed)
    keys = iter(jax.random.split(key, 64))

    def normal(shape, scale):
        return jax.random.normal(next(keys), shape, jnp.float32) * scale

    def gain(n):
        return 1.0 + 0.02 * jax.random.normal(next(keys), (n,), jnp.float32)

    def w_out(width):
        return normal((width, D_MODEL), 0.5 * width ** -0.5)

    s_in = D_MODEL ** -0.5
    inputs = {}
    inputs['x'] = normal((BATCH, SEQ, D_MODEL), 1.0)
    inputs['mem'] = normal((BATCH, N_MEM, D_MODEL), 1.0)
    inputs['positions'] = (jax.random.randint(next(keys), (BATCH, 1), 0, 1024, dtype=jnp.int32)
                           + jnp.arange(SEQ, dtype=jnp.int32)[None, :])
    inputs['l0_norm'] = gain(D_MODEL)
    inputs['l0_w_in'] = jnp.concatenate([
        normal((D_MODEL, 3 * MIX_WIDTH), s_in),
        normal((D_MODEL, N_HEADS), 0.1 * s_in),
        normal((D_MODEL, MEM_WIDTH + MIX_WIDTH + MEM_WIDTH), s_in)], axis=1)
    inputs['l0_forget_bias'] = 3.0 + 0.5 * jax.random.normal(next(keys), (N_HEADS,), jnp.float32)
    inputs['l0_mem_norm'] = gain(D_MODEL)
    inputs['l0_w_mem_kv'] = normal((D_MODEL, 2 * MEM_WIDTH), s_in)
    inputs['l0_w_out'] = w_out(MIX_WIDTH + MEM_WIDTH)
    inputs['l1_norm'] = gain(D_MODEL)
    inputs['l1_w_in'] = normal((D_MODEL, sum(B_SIZES)), s_in)
    inputs['l1_mem_norm'] = gain(D_MODEL)
    inputs['l1_w_mem_kv'] = normal((D_MODEL, 2 * MEM_WIDTH), s_in)
    inputs['l1_w_out'] = w_out(MIX_WIDTH + MEM_WIDTH)
    inputs['l2_norm'] = gain(D_MODEL)
    inputs['l2_w_in'] = normal((D_MODEL, sum(C_SIZES)), s_in)
    inputs['l2_mem_norm'] = gain(D_MODEL)
    inputs['l2_w_mem_kv'] = normal((D_MODEL, 2 * MEM_WIDTH), s_in)
    inputs['l2_w_out'] = w_out(DIL_WIDTH + MEM_WIDTH)
    inputs['l3_norm'] = gain(D_MODEL)
    inputs['l3_w_in'] = normal((D_MODEL, sum(D_SIZES)), s_in)
    inputs['l3_q_norm'] = gain(Q_LORA)
    inputs['l3_w_uq'] = normal((Q_LORA, N_HEADS * (QK_NOPE + QK_ROPE)), Q_LORA ** -0.5)
    inputs['l3_kv_norm'] = gain(KV_LORA)
    inputs['l3_w_ukv'] = normal((KV_LORA, N_HEADS * (QK_NOPE + V_HEAD)), KV_LORA ** -0.5)
    inputs['l3_mem_norm'] = gain(D_MODEL)
    inputs['l3_w_mem_kv'] = normal((D_MODEL, 2 * MEM_WIDTH), s_in)
    inputs['l3_w_out'] = w_out(N_HEADS * V_HEAD + MEM_WIDTH)
    inputs['final_norm'] = gain(D_MODEL)
    return inputs


def reference(x, mem, positions,
              l0_norm, l0_w_in, l0_forget_bias, l0_mem_norm, l0_w_mem_kv, l0_w_out,
              l1_norm, l1_w_in, l1_mem_norm, l1_w_mem_kv, l1_w_out,
              l2_norm, l2_w_in, l2_mem_norm, l2_w_mem_kv, l2_w_out,
              l3_norm, l3_w_in, l3_q_norm, l3_w_uq, l3_kv_norm, l3_w_ukv, l3_mem_norm, l3_w_mem_kv, l3_w_out,
              final_norm):
    mixers = (
        lambda h: forgetting_mixer(h, positions, l0_w_in, l0_forget_bias),
        lambda h: dsa_mixer(h, positions, l1_w_in),
        lambda h: dilated_mixer(h, positions, l2_w_in),
        lambda h: mla_mixer(h, positions, l3_w_in, l3_q_norm, l3_w_uq, l3_kv_norm, l3_w_ukv),
    )
    layer_params = (
        (l0_norm, l0_mem_norm, l0_w_mem_kv, l0_w_out),
        (l1_norm, l1_mem_norm, l1_w_mem_kv, l1_w_out),
        (l2_norm, l2_mem_norm, l2_w_mem_kv, l2_w_out),
        (l3_norm, l3_mem_norm, l3_w_mem_kv, l3_w_out),
    )
    for i in range(DEPTH):
        norm_g, mem_g, w_mem_kv, w_out = layer_params[i]
        x = hybrid_layer(x, mem, norm_g, mem_g, w_mem_kv, w_out, mixers[i % N_MIXERS])
    return rms_norm(x, final_norm)
```

```python
import math
from contextlib import ExitStack

import numpy as np
import concourse.bass as bass
import concourse.mybir as mybir
from concourse.bass_utils import run_bass_kernel_spmd

F32 = mybir.dt.float32
BF16 = mybir.dt.bfloat16
I32 = mybir.dt.int32
AF = mybir.ActivationFunctionType
ALU = mybir.AluOpType
AX = mybir.AxisListType

D = 2048
SEQ = 2048
NT = SEQ // 128
KC = D // 128
NMEM = 256
EPS = 1e-6
PI = math.pi

A_SIZES = (2048, 2048, 2048, 16, 512, 2560)
B_SIZES = (2048, 128, 128, 1024, 64, 16, 512, 2560)
C_SIZES = (2304, 2304, 2304, 512, 1280)
D_SIZES = (512, 512, 64, 512, 2560)


def offs(sizes):
    o = [0]
    for s in sizes:
        o.append(o[-1] + s)
    return o


class Tk:
    __slots__ = ("ap", "w", "r", "dsem", "name")

    def __init__(self, ap, name=""):
        self.ap = ap
        self.w = None
        self.r = {}
        self.dsem = None
        self.name = name

    def __getitem__(self, idx):
        return self.ap[idx]


class Sched:
    ENGS = ("pe", "act", "dve", "pool", "sp")

    def __init__(self, nc, stack):
        self.nc = nc
        self.stack = stack
        self.streams = {e: [] for e in self.ENGS}
        self.sems = {}
        self.count = {}
        self.seen = {e: {} for e in self.ENGS}
        self.nsem = 0
        for e in ("pe", "act", "dve", "pool"):
            self.esem(e)
        self.pending_st = {}
        self.ninstr = 0
        self.free_dsems = {"sw": [], "hw": []}

    def release(self, tk):
        if tk.dsem is not None:
            for cls, key in tk.dsem.items():
                self.free_dsems[cls].append(key)
            tk.dsem = None

    def esem(self, key):
        if key not in self.sems:
            self.sems[key] = self.stack.enter_context(self.nc.semaphore(f"s{self.nsem}"))
            self.nsem += 1
            self.count[key] = 0
        return self.sems[key]

    def _wait(self, eng, tok):
        if tok is None:
            return
        key, val = tok
        if self.seen[eng].get(key, 0) >= val:
            return
        if key == eng and eng == "pe":
            return
        self.seen[eng][key] = val
        sem = self.sems[key]
        self.streams[eng].append(lambda h, sem=sem, val=val: h.wait_ge(sem, val))
        self.ninstr += 1

    def _deps(self, eng, reads, writes):
        for t in reads:
            self._wait(eng, t.w)
        for t in writes:
            self._wait(eng, t.w)
            for k, v in t.r.items():
                self._wait(eng, (k, v))

    def op(self, eng, fn, reads=(), writes=()):
        self._deps(eng, reads, writes)
        self.count[eng] += 1
        n = self.count[eng]
        sem = self.sems[eng]
        self.streams[eng].append(lambda h, fn=fn, sem=sem: fn(h).then_inc(sem, 1))
        self.ninstr += 1
        for t in reads:
            if t.r.get(eng, 0) < n:
                t.r[eng] = n
        for t in writes:
            t.w = (eng, n)
            t.r = {}

    def dma(self, q, out, in_, reads=(), writes=(), dram_write=False):
        self._deps(q, reads, writes)
        t = writes[0] if writes else reads[0]
        cls = "sw" if q == "pool" else "hw"
        if t.dsem is None:
            t.dsem = {}
        if cls not in t.dsem:
            if self.free_dsems[cls]:
                t.dsem[cls] = self.free_dsems[cls].pop()
            else:
                t.dsem[cls] = f"d{self.nsem}"
                self.esem(t.dsem[cls])
        key = t.dsem[cls]
        self.count[key] += 16
        val = self.count[key]
        sem = self.sems[key]
        self.streams[q].append(
            lambda h, out=out, in_=in_, sem=sem: h.dma_start(out=out, in_=in_).then_inc(sem, 16))
        self.ninstr += 1
        for tt in writes:
            tt.w = (key, val)
            tt.r = {}
        for tt in reads:
            if tt.r.get(key, 0) < val:
                tt.r[key] = val
        if dram_write:
            self.pending_st[key] = val

    def barrier(self):
        toks = [(e, self.count[e]) for e in ("pe", "act", "dve", "pool") if self.count[e] > 0]
        toks += list(self.pending_st.items())
        for e in self.ENGS:
            for tok in toks:
                if tok[0] == e:
                    continue
                self._wait(e, tok)
        self.pending_st = {}

    def emit(self):
        nc = self.nc
        with nc.Block() as block:
            @block.tensor
            def _(h):
                for f in self.streams["pe"]:
                    f(h)

            @block.scalar
            def _(h):
                for f in self.streams["act"]:
                    f(h)

            @block.vector
            def _(h):
                for f in self.streams["dve"]:
                    f(h)

            @block.gpsimd
            def _(h):
                for f in self.streams["pool"]:
                    f(h)

            @block.sync
            def _(h):
                for f in self.streams["sp"]:
                    f(h)


MASK_NAMES = ("tri", "low", "m4", "m4tri", "m4low", "m16", "m16tri")
WIDE_NAMES = ("m4", "m16")
MASK_COLS = len(MASK_NAMES) * 128 + len(WIDE_NAMES) * 512


def host_consts():
    import ml_dtypes
    kp = np.arange(128)[:, None]
    qf = np.arange(128)[None, :]
    d = qf - kp
    masks = {
        "tri": d >= 0,
        "low": d <= 0,
        "m4": d % 4 == 0,
        "m4tri": (d % 4 == 0) & (d >= 0),
        "m4low": (d % 4 == 0) & (d <= 0),
        "m16": d % 16 == 0,
        "m16tri": (d % 16 == 0) & (d >= 0),
    }
    mk = np.stack([masks[n] for n in MASK_NAMES], axis=1).astype(np.float32)
    mk = mk.reshape(128, len(MASK_NAMES) * 128)
    wide = [np.tile(masks[n].astype(np.float32), (1, 4)) for n in WIDE_NAMES]
    mk = np.concatenate([mk] + wide, axis=1).astype(ml_dtypes.bfloat16)
    ident = np.eye(128, dtype=np.float32).astype(ml_dtypes.bfloat16)
    cf = np.zeros((128, 8), np.float32)
    j = np.arange(128)
    cf[:, 0] = np.power(np.float32(10000.0), -(j % 64).astype(np.float32) * np.float32(2.0) / np.float32(128))
    cf[:64, 1] = np.power(np.float32(10000.0), -(j[:64] % 32).astype(np.float32) * np.float32(2.0) / np.float32(64))
    cf[:, 2] = np.where(j < 64, -1.0, 1.0)
    cf[:64, 3] = np.where(j[:64] < 32, -1.0, 1.0)
    cf[:, 4] = -PI
    tri32 = (kp <= qf).astype(np.float32)
    ones32 = np.ones((128, 128), np.float32)
    return {"c_masks": mk, "c_ident": ident, "c_f": cf, "c_tri32": tri32, "c_ones32": ones32}


class Builder:
    def __init__(self, layers=(0, 1, 2, 3), final=True, debug=None, own3=False):
        self.own3 = own3 and (3 in layers) and layers[-1] == 3
        self.layers = layers
        self.final = final
        self.debug = debug
        self.uid = 0
        self.nc = bass.Bass("TRN2", target_bir_lowering=False)
        self.st = ExitStack()

    def name(self, base):
        self.uid += 1
        return f"{base}_{self.uid}"

    def sb(self, stack, shape, dt, name="t"):
        tk = Tk(stack.enter_context(self.nc.sbuf_tensor(self.name(name), list(shape), dt)), name)
        if stack is not self.st:
            stack.callback(self.S.release, tk)
        return tk

    def dram_in(self, name, shape, dt):
        return self.nc.dram_tensor(name, list(shape), dt, kind="ExternalInput").ap()

    def dram(self, name, shape, dt):
        return self.nc.dram_tensor(name, list(shape), dt).ap()

    def build(self):
        nc = self.nc
        st = self.st
        self.S = Sched(nc, st)
        S = self.S
        I = {}
        I["x"] = self.dram_in("x", [SEQ, D], F32)
        I["mem"] = self.dram_in("mem", [NMEM, D], F32)
        I["pos"] = self.dram_in("pos", [1, SEQ], I32)
        wshapes = {
            "l0_w_in": [D, 9232], "l1_w_in": [D, 6480], "l2_w_in": [D, 8704], "l3_w_in": [D, 4160],
            "l0_w_out": [2560, D], "l1_w_out": [2560, D], "l2_w_out": [1280, D], "l3_w_out": [2560, D],
            "l3_w_uq": [512, 3072], "l3_w_ukv": [512, 4096],
        }
        for l in range(4):
            wshapes[f"l{l}_w_mem_kv"] = [D, 1024]
            I[f"l{l}_norm"] = self.dram_in(f"l{l}_norm", [1, D], F32)
            I[f"l{l}_mem_norm"] = self.dram_in(f"l{l}_mem_norm", [1, D], F32)
        for k, shp in wshapes.items():
            I[k] = self.dram_in(k, shp, F32)
        I["l0_forget_bias"] = self.dram_in("l0_forget_bias", [1, 16], F32)
        I["l3_q_norm"] = self.dram_in("l3_q_norm", [1, 512], F32)
        I["l3_kv_norm"] = self.dram_in("l3_kv_norm", [1, 512], F32)
        I["final_norm"] = self.dram_in("final_norm", [1, D], F32)
        I["c_masks"] = self.dram_in("c_masks", [128, MASK_COLS], BF16)
        I["c_ident"] = self.dram_in("c_ident", [128, 128], BF16)
        I["c_f"] = self.dram_in("c_f", [128, 8], F32)
        I["c_tri32"] = self.dram_in("c_tri32", [128, 128], F32)
        I["c_ones32"] = self.dram_in("c_ones32", [128, 128], F32)
        self.I = I
        I["c_sel"] = self.dram_in("c_sel", [128, 2], F32)
        I["c_own"] = self.dram_in("c_own", [128, 256], BF16)
        self.out = nc.dram_tensor("out", [SEQ // 2 if self.own3 else SEQ, D], F32, kind="ExternalOutput").ap()

        R = {}
        R["x"] = self.dram("r_x", [SEQ, D], F32)
        R["qT"] = self.dram("r_qT", [18, 128, SEQ], BF16)
        R["kT"] = self.dram("r_kT", [18, 128, SEQ], BF16)
        R["v"] = self.dram("r_v", [18, SEQ, 128], BF16)
        R["zT"] = self.dram("r_zT", [20, 128, SEQ], BF16)
        R["qmT"] = self.dram("r_qmT", [4, 128, SEQ], BF16)
        R["kmT"] = self.dram("r_kmT", [4, 4, 128, NMEM], BF16)
        R["vm"] = self.dram("r_vm", [4, 4, NMEM, 128], BF16)
        R["gT"] = self.dram("r_gT", [20, 128, SEQ], BF16)
        R["q64T"] = self.dram("r_q64T", [16, 64, SEQ], BF16)
        R["mT"] = self.dram("r_mT", [NT, 128, SEQ], BF16)
        R["xo"] = self.dram("r_xo", [SEQ // 2, D], F32)
        self.R = R

        self.ps = [Tk(st.enter_context(nc.psum_tensor(f"ps{i}", [128, 512], F32)), f"ps{i}") for i in range(8)]
        self.rot = {}

        C = {}
        C["masks"] = self.sb(st, [128, MASK_COLS], BF16, "masks")
        C["ident"] = self.sb(st, [128, 128], BF16, "ident")
        C["cf"] = self.sb(st, [128, 8], F32, "cf")
        C["ones"] = self.sb(st, [128, 128], BF16, "ones")
        S.dma("sp", C["masks"][:], I["c_masks"], writes=[C["masks"]])
        S.dma("sp", C["ident"][:], I["c_ident"], writes=[C["ident"]])
        S.dma("sp", C["cf"][:], I["c_f"], writes=[C["cf"]])
        S.op("dve", lambda h: h.memset(C["ones"][:], 1.0), writes=[C["ones"]])
        C["sel"] = self.sb(st, [128, 2], F32, "sel")
        C["own"] = self.sb(st, [128, 256], BF16, "ownm")
        S.dma("sp", C["sel"][:], I["c_sel"], writes=[C["sel"]])
        S.dma("sp", C["own"][:], I["c_own"], writes=[C["own"]])
        self.C = C
        self.rope_alloc()

        self.mem_kv_all()
        xin = I["x"]
        for l in self.layers:
            if l == 3 and self.own3:
                self.layer3_own(xin, R["xo"])
                xin = R["xo"]
            else:
                getattr(self, f"layer{l}")(xin, R["x"])
                xin = R["x"]
        ntile_out = NT // 2 if self.own3 else NT
        if self.final:
            self.final_norm(xin, ntile_out)
        else:
            self.copy_out(xin, ntile_out)
        S.barrier()
        S.emit()
        return nc

    def bank(self, pool):
        pools = {"s": (0, 1, 2, 7), "o": (3, 4), "d": (5, 6), "x": (7,), "g": (0, 1, 2, 3), "t": (4, 5), "y": (6, 7)}
        ids = pools[pool]
        k = self.rot.get(pool, 0)
        self.rot[pool] = k + 1
        return self.ps[ids[k % len(ids)]]

    def mask(self, name):
        i = MASK_NAMES.index(name)
        return self.C["masks"][:, i * 128:(i + 1) * 128]

    def wide_mask(self, name, n):
        o = len(MASK_NAMES) * 128 + WIDE_NAMES.index(name) * 512
        return self.C["masks"][:, o:o + n * 128]

    def rope_alloc(self):
        C, st = self.C, self.st
        for nm in ("128", "64"):
            C["cos" + nm] = self.sb(st, [128, SEQ], F32, "cos" + nm)
            C["sin" + nm] = self.sb(st, [128, SEQ], F32, "sin" + nm)

    def rope_ops(self, ph):
        S, C = self.S, self.C
        posi = self.sb(ph, [128, SEQ], I32, "posi")
        posf = self.sb(ph, [128, SEQ], F32, "posf")
        ang = self.sb(ph, [128, SEQ], F32, "ang")
        tmp = self.sb(ph, [128, SEQ], F32, "tmp")
        tmp2 = self.sb(ph, [128, SEQ], F32, "tmp2")
        ops = []

        def add(eng, fn, reads, writes):
            ops.append(lambda: S.op(eng, fn, reads=reads, writes=writes))
        ops.append(lambda: S.dma("sp", posi[:], self.I["pos"].partition_broadcast(128), writes=[posi]))
        add("dve", lambda h: h.tensor_copy(posf[:], posi[:]), [posi], [posf])
        cf = C["cf"]
        for nm, col, sgn, npart in (("128", 0, 2, 128), ("64", 1, 3, 64)):
            cosT, sinT = C["cos" + nm], C["sin" + nm]
            pp = slice(0, npart)
            add("dve", lambda h, pp=pp, col=col: h.tensor_scalar(
                ang[pp, :], posf[pp, :], cf[pp, col:col + 1], None, ALU.mult), [posf, cf], [ang])
            for dst, shift in ((sinT, 0.0), (cosT, 0.5 * PI)):
                add("dve", lambda h, pp=pp, shift=shift: h.tensor_scalar(
                    tmp[pp, :], ang[pp, :], shift, 1.0 / (2 * PI), ALU.add, ALU.mult), [ang], [tmp])
                add("dve", lambda h, pp=pp: h.tensor_copy(posi[pp, :], tmp[pp, :]), [tmp], [posi])
                add("dve", lambda h, pp=pp: h.tensor_copy(tmp[pp, :], posi[pp, :]), [posi], [tmp])
                add("dve", lambda h, pp=pp, shift=shift: h.tensor_scalar(
                    tmp2[pp, :], ang[pp, :], shift, None, ALU.add), [ang], [tmp2])
                add("dve", lambda h, pp=pp: h.scalar_tensor_tensor(
                    tmp[pp, :], tmp[pp, :], -2 * PI, tmp2[pp, :], ALU.mult, ALU.add), [tmp, tmp2], [tmp])
                add("dve", lambda h, pp=pp: h.tensor_scalar(
                    tmp2[pp, :], tmp[pp, :], PI, -2 * PI, ALU.is_gt, ALU.mult), [tmp], [tmp2])
                add("dve", lambda h, pp=pp: h.tensor_tensor(
                    tmp[pp, :], tmp[pp, :], tmp2[pp, :], ALU.add), [tmp, tmp2], [tmp])
                add("dve", lambda h, pp=pp: h.tensor_scalar(
                    tmp[pp, :], tmp[pp, :], -PI, PI, ALU.max, ALU.min), [tmp], [tmp])
                add("act", lambda h, pp=pp, dst=dst: h.activation(
                    dst[pp, :], tmp[pp, :], AF.Sin), [tmp], [dst])
            add("dve", lambda h, pp=pp, sgn=sgn, sinT=sinT: h.tensor_scalar(
                sinT[pp, :], sinT[pp, :], cf[pp, sgn:sgn + 1], None, ALU.mult), [sinT, cf], [sinT])
        return ops

    def norm_T(self, ph, src, ntile, g_dram, hT_views, hT_ap, scr=None, nb=3):
        S, C = self.S, self.C
        if scr is None:
            scr = {}
        if "gB" not in scr:
            scr["gB"] = self.sb(ph, [128, D], F32, "gB")
            scr["xts"] = [self.sb(ph, [128, D], F32, "xt") for _ in range(3)]
            scr["hbs"] = [self.sb(ph, [128, D], BF16, "hb") for _ in range(nb)]
            scr["junk"] = self.sb(ph, [128, D], BF16, "junk")
            scr["sss"] = [self.sb(ph, [128, 2], F32, "ss") for _ in range(nb)]
        gB, xts, hbs, junk, sss = scr["gB"], scr["xts"], scr["hbs"], scr["junk"], scr["sss"]
        S.dma("sp", gB[:], g_dram.partition_broadcast(128), writes=[gB])
        def stage1(i):
            xt, hb, ss = xts[i % 3], hbs[i % len(hbs)], sss[i % len(sss)]
            S.dma("sp", xt[:], src[i * 128:(i + 1) * 128, :], writes=[xt])
            S.op("act", lambda h, xt=xt, ss=ss: h.activation(junk[:], xt[:], AF.Square, scale=D ** -0.5, accum_out=ss[:, 0:1]),
                 reads=[xt], writes=[junk, ss])
            S.op("dve", lambda h, ss=ss: h.tensor_scalar(ss[:, 1:2], ss[:, 0:1], EPS, None, ALU.add),
                 reads=[ss], writes=[ss])
            S.op("act", lambda h, ss=ss: h.activation(ss[:, 1:2], ss[:, 1:2], AF.Sqrt), reads=[ss], writes=[ss])
            S.op("dve", lambda h, ss=ss: h.reciprocal(ss[:, 1:2], ss[:, 1:2]), reads=[ss], writes=[ss])
            S.op("dve", lambda h, xt=xt, hb=hb, ss=ss: h.scalar_tensor_tensor(
                hb[:], xt[:], ss[:, 1:2], gB[:], ALU.mult, ALU.mult), reads=[xt, ss, gB], writes=[hb])

        def stage2(i):
            hb = hbs[i % len(hbs)]
            for half in range(2):
                pb = self.bank("t")
                pbv = pb.ap.bitcast(BF16)
                for k in range(8):
                    kc = half * 8 + k
                    S.op("pe", lambda h, pbv=pbv, k=k, kc=kc, hb=hb: h.transpose(
                        pbv[:, k * 128:(k + 1) * 128], hb[:, kc * 128:(kc + 1) * 128], C["ident"][:]),
                        reads=[hb, C["ident"]], writes=[pb])
                eng = "act" if half == 0 else "dve"
                dst = hT_ap[:, half * 8:(half + 1) * 8, i * 128:(i + 1) * 128]
                srcv = pbv.rearrange("p (k t) -> p k t", k=8)
                if eng == "act":
                    S.op("act", lambda h, dst=dst, srcv=srcv: h.copy(dst, srcv), reads=[pb], writes=[hT_views[i]])
                else:
                    S.op("dve", lambda h, dst=dst, srcv=srcv: h.tensor_copy(dst, srcv), reads=[pb], writes=[hT_views[i]])

        stage1(0)
        for i in range(ntile):
            if i + 1 < ntile:
                stage1(i + 1)
            stage2(i)

    def gemm(self, ph, src_ap, src_views, ntok, kcn, w_ap, jobs, nbuf=3, wbufs=None):
        S = self.S
        panels = []
        cur = None
        for jb in jobs:
            if "panel" in jb:
                if cur is not None and cur.get("key") == jb["panel"]:
                    cur["jobs"].append(jb)
                else:
                    cur = {"c0": jb["panel"][0], "c1": jb["panel"][1], "jobs": [jb], "key": jb["panel"]}
                    panels.append(cur)
                continue
            if cur is not None and "key" not in cur and jb["c0"] == cur["c1"] and jb["c0"] + jb["nc"] - cur["c0"] <= 512:
                cur["jobs"].append(jb)
                cur["c1"] = jb["c0"] + jb["nc"]
            else:
                cur = {"c0": jb["c0"], "c1": jb["c0"] + jb["nc"], "jobs": [jb]}
                panels.append(cur)
        if wbufs is None:
            wbufs = [self.sb(ph, [128, kcn, 512], BF16, "wp") for _ in range(nbuf)]
        nbuf = len(wbufs)
        wsrc = w_ap.rearrange("(kc p) c -> p kc c", p=128)
        tchunk = min(512, ntok)
        def load(pi):
            pn = panels[pi]
            wp = wbufs[pi % nbuf]
            cw = pn["c1"] - pn["c0"]
            S.dma("pool", wp[:, :, 0:cw], wsrc[:, :, pn["c0"]:pn["c1"]], writes=[wp])

        for pi in range(min(nbuf - 1, len(panels))):
            load(pi)
        for pi, pn in enumerate(panels):
            wp = wbufs[pi % nbuf]
            if pi + nbuf - 1 < len(panels):
                load(pi + nbuf - 1)
            for jb in pn["jobs"]:
                o = jb["c0"] - pn["c0"]
                n = jb["nc"]
                if jb["mode"] == "F":
                    for t0 in range(0, ntok, tchunk):
                        t1 = t0 + tchunk
                        pb = self.bank("g")
                        views = src_views[t0 // 128:(t1 + 127) // 128]
                        for kc in range(kcn):
                            S.op("pe", lambda h, pb=pb, wp=wp, kc=kc, o=o, n=n, t0=t0, t1=t1: h.matmul(
                                pb[0:n, 0:t1 - t0], wp[:, kc, o:o + n], src_ap[:, kc, t0:t1],
                                start=(kc == 0), stop=(kc == kcn - 1)), reads=[wp] + views, writes=[pb])
                        jb["post"](jb, t0, t1, pb)
                else:
                    for i in range(ntok // 128):
                        pb = self.bank("g")
                        rsel = jb.get("rsel")
                        for kc in range(kcn):
                            if rsel is None:
                                S.op("pe", lambda h, pb=pb, wp=wp, kc=kc, o=o, n=n, i=i: h.matmul(
                                    pb[:, 0:n], src_ap[:, kc, i * 128:(i + 1) * 128], wp[:, kc, o:o + n],
                                    start=(kc == 0), stop=(kc == kcn - 1)), reads=[wp, src_views[i]], writes=[pb])
                            else:
                                S.op("pe", lambda h, pb=pb, wp=wp, kc=kc, i=i, rsel=rsel, jb=jb: h.matmul(
                                    jb["osel"](pb), src_ap[:, kc, i * 128:(i + 1) * 128], rsel(wp, kc),
                                    start=(kc == 0), stop=(kc == kcn - 1)), reads=[wp, src_views[i]], writes=[pb])
                        jb["post"](jb, i, pb)
                    if "flush" in jb:
                        jb["flush"]()

    def stager(self, ph, nbuf=3, width=SEQ, dt=BF16, name="stg"):
        bufs = [self.sb(ph, [128, width], dt, name) for _ in range(nbuf)]
        state = {"i": 0}

        def nxt():
            b = bufs[state["i"] % nbuf]
            state["i"] += 1
            return b
        return nxt

    def shared(self, ph, rope=False):
        sh = {"stg": self.stager(ph)}
        if rope:
            sh["tmpA"] = [self.sb(ph, [128, 512], F32, "rtA") for _ in range(3)]
            sh["tmpB"] = [self.sb(ph, [128, 512], F32, "rtB") for _ in range(3)]
        return sh

    def post_F_store(self, ph, dst_of_job, kind, npart=128, stg=None, sh=None, tabs=None):
        S, C = self.S, self.C
        nxt = stg or sh["stg"]
        tmpA = sh.get("tmpA") if sh else None
        tmpB = sh.get("tmpB") if sh else None
        state = {"cur": None, "n": 0}

        def post(jb, t0, t1, pb):
            if t0 == 0:
                state["cur"] = nxt()
            stgt = state["cur"]
            pp = slice(0, npart)
            if kind == "copy":
                eng = "act" if (state["n"] % 2 == 0) else "dve"
                if eng == "act":
                    S.op("act", lambda h: h.copy(stgt[pp, t0:t1], pb[pp, 0:t1 - t0]), reads=[pb], writes=[stgt])
                else:
                    S.op("dve", lambda h: h.tensor_copy(stgt[pp, t0:t1], pb[pp, 0:t1 - t0]), reads=[pb], writes=[stgt])
            elif kind == "silu":
                S.op("act", lambda h: h.activation(stgt[pp, t0:t1], pb[pp, 0:t1 - t0], AF.Silu), reads=[pb], writes=[stgt])
            else:
                hf = npart // 2
                cosT, sinT = (C["cos128"], C["sin128"]) if kind == "rope128" else (C["cos64"], C["sin64"])
                if tabs is not None:
                    cosT, sinT = tabs
                ta, tb = tmpA[state["n"] % 3], tmpB[state["n"] % 3]
                w = t1 - t0
                S.op("dve", lambda h: h.tensor_tensor(ta[0:hf, 0:w], pb[hf:npart, 0:w], sinT[0:hf, t0:t1], ALU.mult),
                     reads=[pb, sinT], writes=[ta])
                S.op("dve", lambda h: h.tensor_tensor(ta[hf:npart, 0:w], pb[0:hf, 0:w], sinT[hf:npart, t0:t1], ALU.mult),
                     reads=[pb, sinT], writes=[ta])
                S.op("dve", lambda h: h.tensor_tensor(tb[pp, 0:w], pb[pp, 0:w], cosT[pp, t0:t1], ALU.mult),
                     reads=[pb, cosT], writes=[tb])
                eng = "dve" if state["n"] % 2 == 0 else "pool"
                S.op(eng, lambda h: h.tensor_tensor(stgt[pp, t0:t1], ta[pp, 0:w], tb[pp, 0:w], ALU.add),
                     reads=[ta, tb], writes=[stgt])
            state["n"] += 1
            if t1 == SEQ or (jb.get("ntok") and t1 == jb["ntok"]):
                dst = dst_of_job(jb)
                S.dma("sp", dst, stgt[pp, 0:t1], reads=[stgt], dram_write=True)
        return post

    def post_T_store(self, ph, dst_fn, stg_width=512, stg=None):
        S = self.S
        nxt = stg or self.stager(ph, nbuf=3, width=stg_width, name="stgT")
        state = {"n": 0}

        def post(jb, i, pb):
            stgt = nxt()
            n = jb["nc"]
            if state["n"] % 2 == 0:
                S.op("act", lambda h: h.copy(stgt[:, 0:n], pb[:, 0:n]), reads=[pb], writes=[stgt])
            else:
                S.op("dve", lambda h: h.tensor_copy(stgt[:, 0:n], pb[:, 0:n]), reads=[pb], writes=[stgt])
            state["n"] += 1
            dst, srcv = dst_fn(jb, i, stgt)
            S.dma("sp", dst, srcv, reads=[stgt], dram_write=True)
        return post

    def mem_kv_all(self):
        S, R, I = self.S, self.R, self.I
        with ExitStack() as ph:
            scr = {}
            rops = self.rope_ops(ph)
            per = (len(rops) + len(self.layers) - 1) // len(self.layers)
            pF_stg = self.stager(ph, nbuf=2, width=NMEM, name="stgm")
            mTs = {l: self.sb(ph, [128, KC, NMEM], BF16, "memT") for l in self.layers}
            wbm = [self.sb(ph, [128, KC, 512], BF16, "wpm") for _ in range(2)]
            pT_st = self.stager(ph, nbuf=3, width=512, name="stgT")
            pT = None
            for l in self.layers:
                mT = mTs[l]
                views = [Tk(None, "memv0"), Tk(None, "memv1")]
                self.norm_T(ph, I["mem"], 2, I[f"l{l}_mem_norm"], views, mT.ap, scr=scr)
                for _ in range(per):
                    if rops:
                        rops.pop(0)()
                jobs = []
                pF = self.post_F_store(ph, (lambda l: (lambda jb: R["kmT"][l][jb["h"]]))(l), "copy", stg=pF_stg)
                for hh in range(4):
                    jobs.append({"mode": "F", "c0": hh * 128, "nc": 128, "h": hh, "post": pF, "ntok": NMEM})

                def dstT(jb, i, stgt, l=l):
                    return (R["vm"][l][:, i * 128:(i + 1) * 128, :].rearrange("h p d -> p h d"),
                            stgt[:, 0:512].rearrange("p (h d) -> p h d", h=4))
                jobs.append({"mode": "T", "c0": 512, "nc": 512, "post": self.post_T_store(ph, dstT, stg=pT_st)})
                pT = True
                self.gemm(ph, mT.ap, views, NMEM, KC, I[f"l{l}_w_mem_kv"], jobs, nbuf=2, wbufs=wbm)
            while rops:
                rops.pop(0)()
            S.barrier()

    def attn_unit(self, W, q0t, nqt, subs, scale, zs_ap, zs_tk, g_out, g_tk, col0, after=None):
        S, C = self.S, self.C
        wq = nqt * 128
        O = self.bank("o")
        Dn = self.bank("d")
        allb = [(sub, blk) for sub in subs for blk in sub["blocks"]]
        nb = len(allb)
        pend = W["pend"]
        for bi, (sub, (j, a, b, masks, biasf)) in enumerate(allb):
            Sp = self.bank("s")
            c0, c1 = a * 128, b * 128
            np_ = len(sub["parts"])
            for pi, (kT, kTk, qT, qTk) in enumerate(sub["parts"]):
                qc = sub["qcol0"]
                S.op("pe", lambda h, Sp=Sp, kT=kT, qT=qT, j=j, c0=c0, c1=c1, qc=qc, pi=pi, np_=np_: h.matmul(
                    Sp[:, c0:c1], kT[:, j * 128:(j + 1) * 128], qT[:, qc + c0:qc + c1],
                    start=(pi == 0), stop=(pi == np_ - 1)), reads=[kTk, qTk], writes=[Sp])
            PT = W["pt"][W["n"] % len(W["pt"])]
            W["n"] += 1
            if biasf is None:
                S.op("act", lambda h, PT=PT, Sp=Sp, c0=c0, c1=c1: h.activation(
                    PT[:, c0:c1], Sp[:, c0:c1], AF.Exp, scale=scale), reads=[Sp], writes=[PT])
            else:
                for u in range(a // 2, (b + 1) // 2):
                    ta, tb_ = max(a, 2 * u), min(b, 2 * u + 2)
                    bap, btk = biasf((q0t + 2 * u) // 2, j)
                    S.op("act", lambda h, PT=PT, Sp=Sp, ta=ta, tb_=tb_, bap=bap: h.activation(
                        PT[:, ta * 128:tb_ * 128], Sp[:, ta * 128:tb_ * 128], AF.Exp, bias=bap, scale=scale),
                        reads=[Sp, btk], writes=[PT])
            if sub.get("blockmask") is not None:
                map_, mtk = sub["blockmask"](j, a, b)
                meng = "dve"
                S.op(meng, lambda h, PT=PT, c0=c0, c1=c1, map_=map_: h.tensor_tensor(
                    PT[:, c0:c1], PT[:, c0:c1], map_, ALU.mult), reads=[PT, mtk], writes=[PT])
            else:
                qt = a
                while qt < b:
                    m = masks[qt - a]
                    if m is None:
                        qt += 1
                        continue
                    map_, mtk = m[0], m[1]
                    n = 1
                    if len(m) > 2 and m[2] in WIDE_NAMES:
                        while qt + n < b and masks[qt + n - a] is not None and len(masks[qt + n - a]) > 2 \
                                and masks[qt + n - a][2] == m[2]:
                            n += 1
                        if n > 1:
                            map_ = self.wide_mask(m[2], n)
                    S.op("dve", lambda h, PT=PT, qt=qt, n=n, map_=map_: h.tensor_tensor(
                        PT[:, qt * 128:(qt + n) * 128], PT[:, qt * 128:(qt + n) * 128], map_, ALU.mult),
                        reads=[PT, mtk], writes=[PT])
                    qt += n
            vap, vtk = sub["v_ap"], sub["v_tk"]

            def pv(O=O, Dn=Dn, vap=vap, vtk=vtk, j=j, PT=PT, c0=c0, c1=c1, bi=bi):
                S.op("pe", lambda h: h.matmul(
                    O[:, c0:c1], vap[:, j, :], PT[:, c0:c1], start=(bi == 0), stop=(bi == nb - 1)),
                    reads=[vtk, PT], writes=[O])
                S.op("pe", lambda h: h.matmul(
                    Dn[:, c0:c1], C["ones"][:], PT[:, c0:c1], start=(bi == 0), stop=(bi == nb - 1)),
                    reads=[C["ones"], PT], writes=[Dn])
            pend.append(pv)
            while len(pend) > W["skew"]:
                pend.pop(0)()
        rd = W["rd"][W["m"] % 2]
        ob = W["ob"][W["m"] % 2]
        W["m"] += 1

        def fin(O=O, Dn=Dn, rd=rd, ob=ob):
            if W.get("recip_act"):
                S.op("act", lambda h: h.activation(rd[:, 0:wq], Dn[:, 0:wq], AF.Ln), reads=[Dn], writes=[rd])
                S.op("act", lambda h: h.activation(rd[:, 0:wq], rd[:, 0:wq], AF.Exp, scale=-1.0), reads=[rd], writes=[rd])
            else:
                S.op("dve", lambda h: h.reciprocal(rd[:, 0:wq], Dn[:, 0:wq]), reads=[Dn], writes=[rd])
            S.op("dve", lambda h: h.tensor_tensor(ob[:, 0:wq], O[:, 0:wq], rd[:, 0:wq], ALU.mult), reads=[O, rd], writes=[ob])
            S.op("pool", lambda h: h.tensor_tensor(g_out, ob[:, 0:wq], zs_ap, ALU.mult), reads=[ob, zs_tk], writes=[g_tk])
            if after is not None:
                after()
        pend.append(fin)

    def attn_flush(self, W):
        while W["pend"]:
            W["pend"].pop(0)()

    def attn_work(self, ph, recip_act=False, skew=3):
        return {"recip_act": recip_act,
                "pt": [self.sb(ph, [128, 512], BF16, "pt") for _ in range(2 * skew)],
                "rd": [self.sb(ph, [128, 512], F32, "rd") for _ in range(2)],
                "ob": [self.sb(ph, [128, 512], F32, "ob") for _ in range(2)],
                "n": 0, "m": 0, "pend": [], "skew": skew}

    def causal_blocks(self, c, biasf=None, allmask=None):
        blocks = []
        for j in range(4 * c + 4):
            a = max(0, j - 4 * c)
            masks = []
            for qt in range(a, 4):
                if allmask is not None:
                    masks.append(allmask(j, 4 * c + qt))
                elif 4 * c + qt == j:
                    masks.append((self.mask("tri"), self.C["masks"]))
                else:
                    masks.append(None)
            blocks.append((j, a, 4, masks, biasf))
        return blocks

    def attn_heads(self, ph, W, heads, scale, gchunk0, load_q64=None, nq=SEQ, gT=None, gviews=None):
        S, R = self.S, self.R
        nbuf = 2
        nsub = max(len(hd) for hd in heads)
        qb = [[self.sb(ph, [128, SEQ], BF16, "qb") for _ in range(nsub)] for _ in range(nbuf)]
        kb = [[self.sb(ph, [128, SEQ], BF16, "kb") for _ in range(nsub)] for _ in range(nbuf)]
        vb = [[self.sb(ph, [128, NT, 128], BF16, "vb") for _ in range(nsub)] for _ in range(nbuf)]
        zb = [self.sb(ph, [128, SEQ], BF16, "zb") for _ in range(nbuf)]
        gb = [self.sb(ph, [128, SEQ], BF16, "gb") for _ in range(nbuf)]
        q2 = k2 = None
        if load_q64 is not None:
            q2 = [self.sb(ph, [64, SEQ], BF16, "q2") for _ in range(nbuf)]
        for n, hd in enumerate(heads):
            bsel = n % nbuf
            subs = []
            for si, sp in enumerate(hd):
                q, k, v = qb[bsel][si], kb[bsel][si], vb[bsel][si]
                nk = sp.get("nk", NT)
                S.dma("sp", q[:, 0:nq], sp["q"][:, 0:nq], writes=[q])
                S.dma("sp", k[:, 0:nk * 128], sp["k"], writes=[k])
                S.dma("sp", v[:, 0:nk, :], sp["v"].rearrange("(j p) d -> p j d", p=128), writes=[v])
                parts = [(k.ap, k, q.ap, q)]
                if load_q64 is not None and "q64" in sp:
                    S.dma("sp", q2[bsel][:, 0:nq], sp["q64"][:, 0:nq], writes=[q2[bsel]])
                    k64 = load_q64
                    parts.append((k64.ap, k64, q2[bsel].ap, q2[bsel]))
                subs.append((sp, parts, v))
            z, g = zb[bsel], gb[bsel]
            S.dma("sp", z[:, 0:nq], R["zT"][gchunk0 + n][:, 0:nq], writes=[z])
            nch = nq // 512
            for c in range(nch):
                ss = []
                for sp, parts, v in subs:
                    ss.append({"parts": parts, "v_ap": v.ap, "v_tk": v, "blocks": sp["blocks_fn"](c), "qcol0": c * 512})
                aft = None
                if gT is not None:
                    g_ap, g_tk = gT[:, gchunk0 + n, c * 512:(c + 1) * 512], gviews[gchunk0 + n]
                else:
                    g_ap, g_tk = g[:, c * 512:(c + 1) * 512], g
                    if c == nch - 1:
                        aft = (lambda g=g, n=n: S.dma("pool", R["gT"][gchunk0 + n][:, 0:nq], g[:, 0:nq], reads=[g], dram_write=True))
                self.attn_unit(W, 4 * c, 4, ss, hd[0].get("scale", scale), z[:, c * 512:(c + 1) * 512], z,
                               g_ap, g_tk, c * 512, after=aft)
        self.attn_flush(W)

    def mem_head_specs(self, l):
        R = self.R
        heads = []
        for hh in range(4):
            heads.append([{"q": R["qmT"][hh], "k": R["kmT"][l][hh], "v": R["vm"][l][hh], "nk": 2, "scale": 128 ** -0.5,
                           "blocks_fn": lambda c: [(0, 0, 4, [None] * 4, None), (1, 0, 4, [None] * 4, None)]}])
        return heads

    def mem_heads(self, ph, W, gchunk0, l, nq=SEQ, gT=None, gviews=None):
        self.attn_heads(ph, W, self.mem_head_specs(l), 128 ** -0.5, gchunk0, nq=nq, gT=gT, gviews=gviews)

    def out_proj(self, l, nchunk, xin, xout, gT=None, gviews=None, wpre=None):
        S, R, I = self.S, self.R, self.I
        with ExitStack() as ph:
            if gT is None:
                gT = self.sb(ph, [128, nchunk, SEQ], BF16, "gT")
                for cc in range(nchunk):
                    S.dma("sp", gT[:, cc, :], R["gT"][cc], writes=[gT])
                gviews = [gT]
            xb = [self.sb(ph, [128, 512], F32, "xb") for _ in range(3)]
            if wpre is not None:
                wbufs = [wpre, self.sb(ph, [128, nchunk, 512], BF16, "wo")]
            else:
                wbufs = [self.sb(ph, [128, nchunk, 512], BF16, "wo") for _ in range(2)]
            wsrc = I[f"l{l}_w_out"].rearrange("(kc p) c -> p kc c", p=128)
            n = 0
            for pc in range(4):
                wp = wbufs[pc % 2]
                if not (pc == 0 and wpre is not None):
                    S.dma("pool", wp[:], wsrc[:, :, pc * 512:(pc + 1) * 512], writes=[wp])
                for i in range(NT):
                    xt = xb[n % 3]
                    n += 1
                    S.dma("act", xt[:], xin[i * 128:(i + 1) * 128, pc * 512:(pc + 1) * 512], writes=[xt])
                    pb = self.bank("g")
                    for cc in range(nchunk):
                        S.op("pe", lambda h, pb=pb, cc=cc, i=i, wp=wp: h.matmul(
                            pb[:], gT[:, cc, i * 128:(i + 1) * 128], wp[:, cc, :],
                            start=(cc == 0), stop=(cc == nchunk - 1)), reads=[gviews[cc % len(gviews)], wp], writes=[pb])
                    S.op("dve", lambda h, xt=xt, pb=pb: h.tensor_tensor(xt[:], xt[:], pb[:], ALU.add),
                         reads=[xt, pb], writes=[xt])
                    S.dma("sp", xout[i * 128:(i + 1) * 128, pc * 512:(pc + 1) * 512], xt[:], reads=[xt], dram_write=True)
            S.barrier()

    def final_norm(self, xin, ntile=NT):
        S, I = self.S, self.I
        with ExitStack() as ph:
            gB = self.sb(ph, [128, D], F32, "gBf")
            S.dma("sp", gB[:], I["final_norm"].partition_broadcast(128), writes=[gB])
            xts = [self.sb(ph, [128, D], F32, "xtf") for _ in range(3)]
            junk = self.sb(ph, [128, D], BF16, "junkf")
            sss = [self.sb(ph, [128, 2], F32, "ssf") for _ in range(3)]
            for i in range(ntile):
                xt, ss = xts[i % 3], sss[i % 3]
                S.dma("sp", xt[:], xin[i * 128:(i + 1) * 128, :], writes=[xt])
                S.op("act", lambda h, xt=xt, ss=ss: h.activation(junk[:], xt[:], AF.Square, scale=D ** -0.5, accum_out=ss[:, 0:1]),
                     reads=[xt], writes=[junk, ss])
                S.op("dve", lambda h, ss=ss: h.tensor_scalar(ss[:, 1:2], ss[:, 0:1], EPS, None, ALU.add),
                     reads=[ss], writes=[ss])
                S.op("act", lambda h, ss=ss: h.activation(ss[:, 1:2], ss[:, 1:2], AF.Sqrt), reads=[ss], writes=[ss])
                S.op("dve", lambda h, ss=ss: h.reciprocal(ss[:, 1:2], ss[:, 1:2]), reads=[ss], writes=[ss])
                S.op("dve", lambda h, xt=xt, ss=ss: h.scalar_tensor_tensor(
                    xt[:], xt[:], ss[:, 1:2], gB[:], ALU.mult, ALU.mult), reads=[xt, ss, gB], writes=[xt])
                S.dma("sp", self.out[i * 128:(i + 1) * 128, :], xt[:], reads=[xt], dram_write=True)

    def copy_out(self, xin, ntile=NT):
        S = self.S
        with ExitStack() as ph:
            xts = [self.sb(ph, [128, D], F32, "xtc") for _ in range(2)]
            for i in range(ntile):
                xt = xts[i % 2]
                S.dma("sp", xt[:], xin[i * 128:(i + 1) * 128, :], writes=[xt])
                S.dma("sp", self.out[i * 128:(i + 1) * 128, :], xt[:], reads=[xt], dram_write=True)

    def project(self, l, xin, make_jobs, extra=None):
        S, I = self.S, self.I
        with ExitStack() as ph:
            hT = self.sb(ph, [128, KC, SEQ], BF16, "hT")
            views = [Tk(None, f"hv{i}") for i in range(NT)]
            with ExitStack() as pa:
                self.norm_T(pa, xin, NT, I[f"l{l}_norm"], views, hT.ap)
                S.barrier()
            jobs = make_jobs(ph)
            self.gemm(ph, hT.ap, views, SEQ, KC, I[f"l{l}_w_in"], jobs)
            if extra is not None:
                extra(ph)
            S.barrier()

    def std_jobs(self, ph, sh, o_qm, o_z, nz):
        R = self.R
        jobs = []
        pqm = self.post_F_store(ph, lambda jb: R["qmT"][jb["h"]], "copy", sh=sh)
        for hh in range(4):
            jobs.append({"mode": "F", "c0": o_qm + hh * 128, "nc": 128, "h": hh, "post": pqm})
        pz = self.post_F_store(ph, lambda jb: R["zT"][jb["h"]], "silu", sh=sh)
        for cc in range(nz):
            jobs.append({"mode": "F", "c0": o_z + cc * 128, "nc": 128, "h": cc, "post": pz})
        return jobs

    def v_jobs(self, ph, o_v, nheads):
        R = self.R

        def dstT(jb, i, stgt):
            nh = jb["nc"] // 128
            h0 = jb["h0"]
            return (R["v"][h0:h0 + nh, i * 128:(i + 1) * 128, :].rearrange("h p d -> p h d"),
                    stgt[:, 0:jb["nc"]].rearrange("p (h d) -> p h d", h=nh))
        pv = self.post_T_store(ph, dstT)
        jobs = []
        h0 = 0
        while h0 < nheads:
            nh = min(4, nheads - h0)
            jobs.append({"mode": "T", "c0": o_v + h0 * 128, "nc": nh * 128, "h0": h0, "post": pv})
            h0 += nh
        return jobs

    def layer3(self, xin, xout):
        S, R, I, C = self.S, self.R, self.I, self.C
        o = offs(D_SIZES)
        l = 3
        with ExitStack() as lay:
            cqT = self.sb(lay, [128, 4, SEQ], BF16, "cqT")
            ckvT = self.sb(lay, [128, 4, SEQ], BF16, "ckvT")
            krT = self.sb(lay, [64, SEQ], BF16, "krT")
            cq_views = [Tk(None, f"cqv{i}") for i in range(NT)]
            ckv_views = [Tk(None, f"ckvv{i}") for i in range(NT)]

            def make_jobs(ph):
                jobs = []
                gq = self.sb(ph, [128, 512], F32, "gq")
                gkv = self.sb(ph, [128, 512], F32, "gkv")
                S.dma("sp", gq[:], I["l3_q_norm"].partition_broadcast(128), writes=[gq])
                S.dma("sp", gkv[:], I["l3_kv_norm"].partition_broadcast(128), writes=[gkv])
                junk = self.sb(ph, [128, 512], BF16, "junkl")
                sss = [self.sb(ph, [128, 2], F32, "ssl") for _ in range(2)]
                cbs = [self.sb(ph, [128, 512], BF16, "cb") for _ in range(2)]
                st_ = {"n": 0}

                def mk_post(g, dstT, dviews):
                    def post(jb, i, pb):
                        ss, cb = sss[st_["n"] % 2], cbs[st_["n"] % 2]
                        st_["n"] += 1
                        S.op("act", lambda h: h.activation(junk[:], pb[:, 0:512], AF.Square, scale=512 ** -0.5, accum_out=ss[:, 0:1]),
                             reads=[pb], writes=[junk, ss])
                        S.op("dve", lambda h: h.tensor_scalar(ss[:, 1:2], ss[:, 0:1], EPS, None, ALU.add),
                             reads=[ss], writes=[ss])
                        S.op("act", lambda h: h.activation(ss[:, 1:2], ss[:, 1:2], AF.Sqrt), reads=[ss], writes=[ss])
                        S.op("dve", lambda h: h.reciprocal(ss[:, 1:2], ss[:, 1:2]), reads=[ss], writes=[ss])
                        S.op("dve", lambda h: h.scalar_tensor_tensor(
                            cb[:], pb[:, 0:512], ss[:, 1:2], g[:], ALU.mult, ALU.mult), reads=[pb, ss, g], writes=[cb])
                        tb = self.bank("t")
                        tbv = tb.ap.bitcast(BF16)
                        for k in range(4):
                            S.op("pe", lambda h, k=k: h.transpose(
                                tbv[:, k * 128:(k + 1) * 128], cb[:, k * 128:(k + 1) * 128], C["ident"][:]),
                                reads=[cb, C["ident"]], writes=[tb])
                        S.op("act", lambda h: h.copy(dstT[:, :, i * 128:(i + 1) * 128],
                                                     tbv[:, 0:512].rearrange("p (k t) -> p k t", k=4)),
                             reads=[tb], writes=[dviews[i]])
                    return post
                jobs.append({"mode": "T", "c0": o[0], "nc": 512, "post": mk_post(gq, cqT, cq_views)})
                jobs.append({"mode": "T", "c0": o[1], "nc": 512, "post": mk_post(gkv, ckvT, ckv_views)})
                tmpA = self.sb(ph, [128, 512], F32, "krA")
                tmpB = self.sb(ph, [128, 512], F32, "krB")

                def post_kr(jb, t0, t1, pb):
                    self.rope_block(pb, 64, t0, t1, tmpA, tmpB, krT[0:64, t0:t1], krT)
                jobs.append({"mode": "F", "c0": o[2], "nc": 64, "post": post_kr})
                sh = self.shared(ph)
                jobs += self.std_jobs(ph, sh, o[3], o[4], 20)
                return jobs

            self.project(l, xin, make_jobs)

            with ExitStack() as ph:
                jobs = []
                sh = self.shared(ph, rope=True)
                pq = self.post_F_store(ph, lambda jb: R["qT"][jb["h"]], "copy", sh=sh)
                pqr = self.post_F_store(ph, lambda jb: R["q64T"][jb["h"]], "rope64", npart=64, sh=sh)
                for hh in range(16):
                    jobs.append({"mode": "F", "c0": hh * 192, "nc": 128, "h": hh, "post": pq})
                    jobs.append({"mode": "F", "c0": hh * 192 + 128, "nc": 64, "h": hh, "post": pqr})
                self.gemm(ph, cqT.ap, cq_views, SEQ, 4, I["l3_w_uq"], jobs)
                S.barrier()
            with ExitStack() as ph:
                jobs = []
                sh = self.shared(ph)
                pk = self.post_F_store(ph, lambda jb: R["kT"][jb["h"]], "copy", sh=sh)

                def dstT(jb, i, stgt):
                    return (R["v"][jb["h"], i * 128:(i + 1) * 128, :], stgt[:, 0:128])
                pv = self.post_T_store(ph, dstT)
                for hh in range(16):
                    jobs.append({"mode": "F", "c0": hh * 256, "nc": 128, "h": hh, "post": pk})
                    jobs.append({"mode": "T", "c0": hh * 256 + 128, "nc": 128, "h": hh, "post": pv})
                self.gemm(ph, ckvT.ap, ckv_views, SEQ, 4, I["l3_w_ukv"], jobs)
                S.barrier()

            with ExitStack() as ph:
                W = self.attn_work(ph)
                heads = []
                for hh in range(16):
                    heads.append([{"q": R["qT"][hh], "k": R["kT"][hh], "v": R["v"][hh], "q64": R["q64T"][hh],
                                   "blocks_fn": lambda c: self.causal_blocks(c)}])
                self.attn_heads(ph, W, heads, 192 ** -0.5, 0, load_q64=krT)
                S.barrier()
            with ExitStack() as ph:
                W = self.attn_work(ph)
                self.mem_heads(ph, W, 16, l)
                S.barrier()
        self.out_proj(l, 20, xin, xout)

    def own_blocks(self, c):
        ownm = self.C["own"]
        blocks = []
        for pk in range(0, 2 * (4 * c + 3) + 2):
            a = max(0, pk // 2 - 4 * c)
            masks = []
            for qt in range(a, 4):
                m = 4 * c + qt
                if pk == 2 * m:
                    masks.append((ownm[:, 0:128], ownm))
                elif pk == 2 * m + 1:
                    masks.append((ownm[:, 128:256], ownm))
                else:
                    masks.append(None)
            blocks.append((pk, a, 4, masks, None))
        return blocks

    def blend_tiles(self, dst_fn, src_fn, dst_tk, src_tk, nt=NT // 2):
        S, sel = self.S, self.C["sel"]
        for m in range(nt):
            np_ = dst_fn(m).shape[0]
            S.op("dve", lambda h, m=m, np_=np_: h.tensor_scalar(
                dst_fn(m), src_fn(2 * m), sel[0:np_, 0:1], None, ALU.mult), reads=[src_tk, sel], writes=[dst_tk])
            S.op("dve", lambda h, m=m, np_=np_: h.scalar_tensor_tensor(
                dst_fn(m), src_fn(2 * m + 1), sel[0:np_, 1:2], dst_fn(m), ALU.mult, ALU.add),
                reads=[src_tk, sel, dst_tk], writes=[dst_tk])

    def layer3_own(self, xin, xout):
        S, R, I, C = self.S, self.R, self.I, self.C
        o = offs(D_SIZES)
        l = 3
        HS = SEQ // 2
        with ExitStack() as lay:
            cqT = self.sb(lay, [128, 4, HS], BF16, "cqT")
            ckvT = self.sb(lay, [128, 4, SEQ], BF16, "ckvT")
            krT = self.sb(lay, [64, SEQ], BF16, "krT")
            cq_views = [Tk(None, f"cqv{i}") for i in range(NT // 2)]
            ckv_views = [Tk(None, f"ckvv{i}") for i in range(NT)]

            def latent_post(ph, g):
                junk = self.sb(ph, [128, 512], BF16, "junkl")
                sss = [self.sb(ph, [128, 2], F32, "ssl") for _ in range(2)]
                cbs = [self.sb(ph, [128, 512], BF16, "cb") for _ in range(2)]
                st_ = {"n": 0}

                def mk_post(dstT, dviews):
                    def post(jb, i, pb):
                        ss, cb = sss[st_["n"] % 2], cbs[st_["n"] % 2]
                        st_["n"] += 1
                        S.op("act", lambda h: h.activation(junk[:], pb[:, 0:512], AF.Square, scale=512 ** -0.5, accum_out=ss[:, 0:1]),
                             reads=[pb], writes=[junk, ss])
                        S.op("dve", lambda h: h.tensor_scalar(ss[:, 1:2], ss[:, 0:1], EPS, None, ALU.add),
                             reads=[ss], writes=[ss])
                        S.op("act", lambda h: h.activation(ss[:, 1:2], ss[:, 1:2], AF.Sqrt), reads=[ss], writes=[ss])
                        S.op("dve", lambda h: h.reciprocal(ss[:, 1:2], ss[:, 1:2]), reads=[ss], writes=[ss])
                        S.op("dve", lambda h: h.scalar_tensor_tensor(
                            cb[:], pb[:, 0:512], ss[:, 1:2], g[:], ALU.mult, ALU.mult), reads=[pb, ss, g], writes=[cb])
                        def stage_b(i=i, cb=cb):
                            tb = self.bank("t")
                            tbv = tb.ap.bitcast(BF16)
                            for k in range(4):
                                S.op("pe", lambda h, k=k: h.transpose(
                                    tbv[:, k * 128:(k + 1) * 128], cb[:, k * 128:(k + 1) * 128], C["ident"][:]),
                                    reads=[cb, C["ident"]], writes=[tb])
                            S.op("act", lambda h: h.copy(dstT[:, :, i * 128:(i + 1) * 128],
                                                         tbv[:, 0:512].rearrange("p (k t) -> p k t", k=4)),
                                 reads=[tb], writes=[dviews[i]])
                        prev = pend_b.pop() if pend_b else None
                        pend_b.append(stage_b)
                        if prev is not None:
                            prev()

                    def flush():
                        while pend_b:
                            pend_b.pop()()
                    post.flush = flush
                    return post
                pend_b = []
                return mk_post

            with ExitStack() as pho:
                hTo = self.sb(pho, [128, KC, HS], BF16, "hTo")
                oviews = [hTo for _ in range(NT // 2)]
                with ExitStack() as ph:
                    hT = self.sb(ph, [128, KC, SEQ], BF16, "hT")
                    views = [Tk(None, f"hv{i}") for i in range(NT)]
                    with ExitStack() as pa:
                        self.norm_T(pa, xin, NT, I[f"l{l}_norm"], views, hT.ap, nb=2)
                        S.barrier()
                    self.blend_tiles(lambda m: hTo[:, :, m * 128:(m + 1) * 128], lambda t: hT[:, :, t * 128:(t + 1) * 128],
                                     hTo, hT)
                    wb = [self.sb(ph, [128, KC, 512], BF16, "wp") for _ in range(2)]
                    gkv = self.sb(ph, [128, 512], F32, "gkv")
                    S.dma("sp", gkv[:], I["l3_kv_norm"].partition_broadcast(128), writes=[gkv])
                    tmpA = self.sb(ph, [128, 512], F32, "krA")
                    tmpB = self.sb(ph, [128, 512], F32, "krB")

                    def post_kr(jb, t0, t1, pb):
                        self.rope_block(pb, 64, t0, t1, tmpA, tmpB, krT[0:64, t0:t1], krT)
                    pkv = latent_post(ph, gkv)(ckvT, ckv_views)
                    jobs = [{"mode": "T", "c0": o[1], "nc": 512, "post": pkv, "flush": pkv.flush},
                            {"mode": "F", "c0": o[2], "nc": 64, "post": post_kr}]
                    self.gemm(ph, hT.ap, views, SEQ, KC, I[f"l{l}_w_in"], jobs, wbufs=wb)
                    S.barrier()
                with ExitStack() as ph:
                    wb = [self.sb(ph, [128, KC, 512], BF16, "wp") for _ in range(3)]
                    gq = self.sb(ph, [128, 512], F32, "gq")
                    S.dma("sp", gq[:], I["l3_q_norm"].partition_broadcast(128), writes=[gq])
                    sh = self.shared(ph)
                    pcq = latent_post(ph, gq)(cqT, cq_views)
                    jobs = [{"mode": "T", "c0": o[0], "nc": 512, "post": pcq, "flush": pcq.flush}]
                    pqm = self.post_F_store(ph, lambda jb: R["qmT"][jb["h"]][:, 0:HS], "copy", sh=sh)
                    for hh in range(4):
                        jobs.append({"mode": "F", "c0": o[3] + hh * 128, "nc": 128, "h": hh, "post": pqm, "ntok": HS})
                    pz = self.post_F_store(ph, lambda jb: R["zT"][jb["h"]][:, 0:HS], "silu", sh=sh)
                    for cc in range(20):
                        jobs.append({"mode": "F", "c0": o[4] + cc * 128, "nc": 128, "h": cc, "post": pz, "ntok": HS})
                    self.gemm(ph, hTo.ap, oviews, HS, KC, I[f"l{l}_w_in"], jobs, wbufs=wb)
                    S.barrier()

            with ExitStack() as ph:
                jobs = []
                cos_o = self.sb(ph, [64, HS], F32, "cos64o")
                sin_o = self.sb(ph, [64, HS], F32, "sin64o")
                self.blend_tiles(lambda m: cos_o[0:64, m * 128:(m + 1) * 128],
                                 lambda t: C["cos64"][0:64, t * 128:(t + 1) * 128], cos_o, C["cos64"])
                self.blend_tiles(lambda m: sin_o[0:64, m * 128:(m + 1) * 128],
                                 lambda t: C["sin64"][0:64, t * 128:(t + 1) * 128], sin_o, C["sin64"])
                sh = self.shared(ph, rope=True)
                pq = self.post_F_store(ph, lambda jb: R["qT"][jb["h"]][:, 0:HS], "copy", sh=sh)
                pqr = self.post_F_store(ph, lambda jb: R["q64T"][jb["h"]][:, 0:HS], "rope64", npart=64, sh=sh,
                                        tabs=(cos_o, sin_o))
                for hh in range(16):
                    jobs.append({"mode": "F", "c0": hh * 192, "nc": 128, "h": hh, "post": pq, "ntok": HS})
                    jobs.append({"mode": "F", "c0": hh * 192 + 128, "nc": 64, "h": hh, "post": pqr, "ntok": HS})
                self.gemm(ph, cqT.ap, cq_views, HS, 4, I["l3_w_uq"], jobs)
                S.barrier()
            with ExitStack() as ph:
                jobs = []
                sh = self.shared(ph)
                pk = self.post_F_store(ph, lambda jb: R["kT"][jb["h"]], "copy", sh=sh)

                def dstT(jb, i, stgt):
                    return (R["v"][jb["h"], i * 128:(i + 1) * 128, :], stgt[:, 0:128])
                def dstT4(jb, i, stgt):
                    h0 = jb["h0"]
                    return (R["v"][h0:h0 + 4, i * 128:(i + 1) * 128, :].rearrange("h p d -> p h d"),
                            stgt[:, 0:512].rearrange("p (h d) -> p h d", h=4))
                pv = self.post_T_store(ph, dstT4)
                wb2 = [self.sb(ph, [128, 4, 1024], BF16, "wp2") for _ in range(3)]
                for h0 in range(0, 16, 4):
                    pan = (h0 * 256, h0 * 256 + 1024)
                    for hh in range(h0, h0 + 4):
                        jobs.append({"mode": "F", "c0": hh * 256, "nc": 128, "h": hh, "post": pk, "panel": pan})
                    jobs.append({"mode": "T", "c0": h0 * 256, "nc": 512, "h0": h0, "post": pv, "panel": pan,
                                 "rsel": lambda wp, kc: wp[:, kc, 0:1024].rearrange("p (h t d) -> p h t d", h=4, t=2)[:, :, 1, :],
                                 "osel": lambda pb: pb[:, 0:512].rearrange("p (h d) -> p h d", h=4)})
                self.gemm(ph, ckvT.ap, ckv_views, SEQ, 4, I["l3_w_ukv"], jobs, wbufs=wb2)
                S.barrier()

            gT = self.sb(lay, [128, 20, HS], BF16, "gT")
            gviews = [Tk(None, f"gv{i}") for i in range(20)]
            with ExitStack() as ph:
                W = self.attn_work(ph)
                heads = []
                for hh in range(16):
                    heads.append([{"q": R["qT"][hh], "k": R["kT"][hh], "v": R["v"][hh], "q64": R["q64T"][hh],
                                   "blocks_fn": lambda c: self.own_blocks(c)}])
                self.attn_heads(ph, W, heads + self.mem_head_specs(l), 192 ** -0.5, 0, load_q64=krT, nq=HS,
                                gT=gT, gviews=gviews)
                S.barrier()
            nchunk = 20
            with ExitStack() as ph:
                xe = [self.sb(ph, [128, 512], F32, "xe") for _ in range(3)]
                xo_ = [self.sb(ph, [128, 512], F32, "xo") for _ in range(3)]
                wbufs = [self.sb(ph, [128, nchunk, 512], BF16, "wo") for _ in range(2)]
                wsrc = I[f"l{l}_w_out"].rearrange("(kc p) c -> p kc c", p=128)
                sel = C["sel"]
                n = 0
                for pc in range(4):
                    wp = wbufs[pc % 2]
                    S.dma("pool", wp[:], wsrc[:, :, pc * 512:(pc + 1) * 512], writes=[wp])
                    for m in range(NT // 2):
                        xa, xb_ = xe[n % 3], xo_[n % 3]
                        n += 1
                        S.dma("act", xa[:], xin[(2 * m) * 128:(2 * m + 1) * 128, pc * 512:(pc + 1) * 512], writes=[xa])
                        S.dma("act", xb_[:], xin[(2 * m + 1) * 128:(2 * m + 2) * 128, pc * 512:(pc + 1) * 512], writes=[xb_])
                        pb = self.bank("g")
                        for cc in range(nchunk):
                            S.op("pe", lambda h, pb=pb, cc=cc, m=m, wp=wp: h.matmul(
                                pb[:], gT[:, cc, m * 128:(m + 1) * 128], wp[:, cc, :],
                                start=(cc == 0), stop=(cc == nchunk - 1)), reads=[gviews[cc], wp], writes=[pb])
                        S.op("dve", lambda h, xa=xa: h.tensor_scalar(xa[:], xa[:], sel[:, 0:1], None, ALU.mult),
                             reads=[xa, sel], writes=[xa])
                        S.op("dve", lambda h, xa=xa, xb_=xb_: h.scalar_tensor_tensor(
                            xa[:], xb_[:], sel[:, 1:2], xa[:], ALU.mult, ALU.add), reads=[xa, xb_, sel], writes=[xa])
                        S.op("dve", lambda h, xa=xa, pb=pb: h.tensor_tensor(xa[:], xa[:], pb[:], ALU.add),
                             reads=[xa, pb], writes=[xa])
                        S.dma("sp", xout[m * 128:(m + 1) * 128, pc * 512:(pc + 1) * 512], xa[:], reads=[xa], dram_write=True)
                S.barrier()

    def rope_block(self, pb, npart, t0, t1, ta, tb, dst_ap, dst_tk):
        S, C = self.S, self.C
        hf = npart // 2
        cosT, sinT = (C["cos128"], C["sin128"]) if npart == 128 else (C["cos64"], C["sin64"])
        w = t1 - t0
        pp = slice(0, npart)
        S.op("dve", lambda h: h.tensor_tensor(ta[0:hf, 0:w], pb[hf:npart, 0:w], sinT[0:hf, t0:t1], ALU.mult),
             reads=[pb, sinT], writes=[ta])
        S.op("dve", lambda h: h.tensor_tensor(ta[hf:npart, 0:w], pb[0:hf, 0:w], sinT[hf:npart, t0:t1], ALU.mult),
             reads=[pb, sinT], writes=[ta])
        S.op("dve", lambda h: h.tensor_tensor(tb[pp, 0:w], pb[pp, 0:w], cosT[pp, t0:t1], ALU.mult),
             reads=[pb, cosT], writes=[tb])
        S.op("pool", lambda h: h.tensor_tensor(dst_ap, ta[pp, 0:w], tb[pp, 0:w], ALU.add),
             reads=[ta, tb], writes=[dst_tk])

    def layer0(self, xin, xout):
        S, R, I, C = self.S, self.R, self.I, self.C
        o = offs(A_SIZES)
        l = 0
        with ExitStack() as lay:
            logf = self.sb(lay, [128, NT, 16], F32, "logf")
            biasall = self.sb(lay, [128, NT // 2, 16, NT], F32, "biasall")
            lf_views = [Tk(None, f"lf{i}") for i in range(NT)]

            def make_jobs(ph):
                jobs = []
                sh = self.shared(ph)
                pq = self.post_F_store(ph, lambda jb: R["qT"][jb["h"]], "copy", sh=sh)
                pk = self.post_F_store(ph, lambda jb: R["kT"][jb["h"]], "copy", sh=sh)
                for hh in range(16):
                    jobs.append({"mode": "F", "c0": o[0] + hh * 128, "nc": 128, "h": hh, "post": pq})
                for hh in range(16):
                    jobs.append({"mode": "F", "c0": o[1] + hh * 128, "nc": 128, "h": hh, "post": pk})
                jobs += self.v_jobs(ph, o[2], 16)
                fb = self.sb(ph, [128, 16], F32, "fb")
                S.dma("sp", fb[:], I["l0_forget_bias"].partition_broadcast(128), writes=[fb])
                ft = self.sb(ph, [128, 16], F32, "ft")

                def post_f(jb, i, pb):
                    S.op("dve", lambda h: h.tensor_tensor(ft[:], pb[:, 0:16], fb[:], ALU.add), reads=[pb, fb], writes=[ft])
                    S.op("act", lambda h: h.activation(ft[:], ft[:], AF.Exp, scale=-1.0), reads=[ft], writes=[ft])
                    S.op("act", lambda h: h.activation(ft[:], ft[:], AF.Ln, bias=1.0, scale=1.0), reads=[ft], writes=[ft])
                    S.op("dve", lambda h: h.tensor_scalar(logf[:, i, :], ft[:], -1.0, None, ALU.mult),
                         reads=[ft], writes=[lf_views[i]])
                jobs.append({"mode": "T", "c0": o[3], "nc": 16, "post": post_f})
                jobs += self.std_jobs(ph, sh, o[4], o[5], 20)
                return jobs

            def extra(ph):
                tri = self.sb(ph, [128, 128], F32, "tri32")
                one = self.sb(ph, [128, 128], F32, "one32")
                S.dma("sp", tri[:], I["c_tri32"], writes=[tri])
                S.dma("sp", one[:], I["c_ones32"], writes=[one])
                cw = self.sb(ph, [128, NT, 16], F32, "cw")
                tot = self.sb(ph, [128, NT + 1, 16], F32, "tot")
                lfa = logf[:].rearrange("p i h -> p (i h)")
                p1, p2 = self.bank("y"), self.bank("y")
                S.op("pe", lambda h: h.matmul(p1[:, 0:256], tri[:], lfa, start=True, stop=True),
                     reads=[tri] + lf_views, writes=[p1])
                S.op("pe", lambda h: h.matmul(p2[:, 0:256], one[:], lfa, start=True, stop=True),
                     reads=[one] + lf_views, writes=[p2])
                S.op("dve", lambda h: h.memset(tot[:, 0, :], 0.0), writes=[tot])
                for i in range(NT):
                    S.op("dve", lambda h, i=i: h.tensor_tensor(tot[:, i + 1, :], tot[:, i, :], p2[:, i * 16:(i + 1) * 16], ALU.add),
                         reads=[tot, p2], writes=[tot])
                S.op("dve", lambda h: h.tensor_tensor(cw[:].rearrange("p i h -> p (i h)"), p1[:, 0:256],
                                                      tot[:, 0:NT, :].rearrange("p i h -> p (i h)"), ALU.add),
                     reads=[p1, tot], writes=[cw])
                for u in range(NT // 2):
                    S.op("dve", lambda h, u=u: h.tensor_tensor(
                        biasall[:, u, :, :], tot[:, 2 * u + 1, :].unsqueeze(2).to_broadcast([128, 16, NT]),
                        cw[:].rearrange("p j h -> p h j"), ALU.subtract), reads=[tot, cw], writes=[biasall])
                    S.op("dve", lambda h, u=u: h.tensor_scalar(
                        biasall[:, u, :, :], biasall[:, u, :, :], 60.0, None, ALU.min), reads=[biasall], writes=[biasall])

            self.project(l, xin, make_jobs, extra)
            gT = self.sb(lay, [128, 20, SEQ], BF16, "gT")
            gviews = [Tk(None, f"gv{i}") for i in range(20)]
            wo0 = self.sb(lay, [128, 20, 512], BF16, "wo0")
            S.dma("pool", wo0[:], I[f"l{l}_w_out"].rearrange("(kc p) c -> p kc c", p=128)[:, :, 0:512], writes=[wo0])

            with ExitStack() as ph:
                W = self.attn_work(ph)
                heads = []
                for hh in range(16):
                    bf = (lambda hh: (lambda i, j: (biasall[:, i, hh, j:j + 1], biasall)))(hh)
                    heads.append([{"q": R["qT"][hh], "k": R["kT"][hh], "v": R["v"][hh],
                                   "blocks_fn": (lambda bf: (lambda c: self.causal_blocks(c, biasf=bf)))(bf)}])
                self.attn_heads(ph, W, heads + self.mem_head_specs(l), 128 ** -0.5, 0, gT=gT, gviews=gviews)
                S.barrier()
            self.out_proj(l, 20, xin, xout, gT=gT, gviews=gviews, wpre=wo0)


    def dil_blocks(self, c, g):
        maxd, diag, mid, edge = ((1, "tri", None, "low"), (4, "m4tri", "m4", "m4low"), (NT, "m16tri", "m16", "m16"))[g]
        M = self.C["masks"]
        blocks = []
        for j in range(max(0, 4 * c - maxd), 4 * c + 4):
            a = max(j, 4 * c) - 4 * c
            b = min(j + maxd, 4 * c + 3) - 4 * c + 1
            masks = []
            for qt in range(a, b):
                dlt = 4 * c + qt - j
                nm = diag if dlt == 0 else (edge if dlt == maxd else mid)
                masks.append((self.mask(nm), M, nm))
            blocks.append((j, a, b, masks, None))
        return blocks

    def layer2(self, xin, xout):
        S, R, I, C = self.S, self.R, self.I, self.C
        o = offs(C_SIZES)
        l = 2

        def make_jobs(ph):
            jobs = []
            sh = self.shared(ph, rope=True)
            pq = self.post_F_store(ph, lambda jb: R["qT"][jb["h"]], "rope128", sh=sh)
            pk = self.post_F_store(ph, lambda jb: R["kT"][jb["h"]], "rope128", sh=sh)
            for hh in range(18):
                jobs.append({"mode": "F", "c0": o[0] + hh * 128, "nc": 128, "h": hh, "post": pq})
            for hh in range(18):
                jobs.append({"mode": "F", "c0": o[1] + hh * 128, "nc": 128, "h": hh, "post": pk})
            jobs += self.v_jobs(ph, o[2], 18)
            jobs += self.std_jobs(ph, sh, o[3], o[4], 10)
            return jobs

        self.project(l, xin, make_jobs)
        lay = ExitStack()
        gT = self.sb(lay, [128, 10, SEQ], BF16, "gT")
        gviews = [Tk(None, f"gv{i}") for i in range(10)]
        wo0 = self.sb(lay, [128, 10, 512], BF16, "wo0")
        S.dma("pool", wo0[:], I[f"l{l}_w_out"].rearrange("(kc p) c -> p kc c", p=128)[:, :, 0:512], writes=[wo0])
        with ExitStack() as ph:
            W = self.attn_work(ph, recip_act=True, skew=4)
            heads = []
            for hh in range(6):
                subs = []
                for g in range(3):
                    n = g * 6 + hh
                    subs.append({"q": R["qT"][n], "k": R["kT"][n], "v": R["v"][n],
                                 "blocks_fn": (lambda g: (lambda c: self.dil_blocks(c, g)))(g)})
                heads.append(subs)
            self.attn_heads(ph, W, heads + self.mem_head_specs(l), 128 ** -0.5, 0, gT=gT, gviews=gviews)
            S.barrier()
        self.out_proj(l, 10, xin, xout, gT=gT, gviews=gviews, wpre=wo0)
        lay.close()

    def layer1(self, xin, xout):
        S, R, I, C = self.S, self.R, self.I, self.C
        o = offs(B_SIZES)
        l = 1
        BIG = 1.0e30
        with ExitStack() as lay:
            kiT = self.sb(lay, [64, SEQ], BF16, "kiT")
            widx = self.sb(lay, [128, NT, 16], F32, "widx")
            wi_views = [Tk(None, f"wi{i}") for i in range(NT)]

            def make_jobs(ph):
                jobs = []
                sh = self.shared(ph, rope=True)
                pq = self.post_F_store(ph, lambda jb: R["qT"][jb["h"]], "rope128", sh=sh)
                pk = self.post_F_store(ph, lambda jb: R["kT"][0], "rope128", sh=sh)
                for hh in range(16):
                    jobs.append({"mode": "F", "c0": o[0] + hh * 128, "nc": 128, "h": hh, "post": pq})
                jobs.append({"mode": "F", "c0": o[1], "nc": 128, "h": 0, "post": pk})
                jobs += self.v_jobs(ph, o[2], 1)
                pqi = self.post_F_store(ph, lambda jb: R["q64T"][jb["h"]], "rope64", npart=64, sh=sh)
                for hh in range(16):
                    jobs.append({"mode": "F", "c0": o[3] + hh * 64, "nc": 64, "h": hh, "post": pqi})
                tA, tB = sh["tmpA"][0], sh["tmpB"][0]

                def post_ki(jb, t0, t1, pb):
                    self.rope_block(pb, 64, t0, t1, tA, tB, kiT[0:64, t0:t1], kiT)
                jobs.append({"mode": "F", "c0": o[4], "nc": 64, "post": post_ki})

                def post_w(jb, i, pb):
                    S.op("act", lambda h: h.copy(widx[:, i, :], pb[:, 0:16]), reads=[pb], writes=[wi_views[i]])
                jobs.append({"mode": "T", "c0": o[5], "nc": 16, "post": post_w})
                jobs += self.std_jobs(ph, sh, o[6], o[7], 20)
                return jobs

            self.project(l, xin, make_jobs)

            with ExitStack() as ph:
                aw = self.sb(ph, [128, NT, 16], F32, "aw")
                sg = self.sb(ph, [128, NT, 16], F32, "sg")
                pm = self.sb(ph, [128, 128], F32, "pm")
                S.op("act", lambda h: h.activation(aw[:], widx[:], AF.Abs), reads=wi_views, writes=[aw])
                S.op("dve", lambda h: h.tensor_scalar(sg[:], widx[:], 0.0, 2.0, ALU.is_ge, ALU.mult), reads=wi_views, writes=[sg])
                S.op("dve", lambda h: h.tensor_scalar(sg[:], sg[:], -1.0, None, ALU.add), reads=[sg], writes=[sg])
                S.op("dve", lambda h: h.tensor_scalar(pm[:], self.mask("low"), 2.0, -1.0, ALU.mult, ALU.add),
                     reads=[C["masks"]], writes=[pm])
                S.op("dve", lambda h: h.tensor_scalar(pm[:], pm[:], BIG, None, ALU.mult), reads=[pm], writes=[pm])
                G = 4
                NIT = 24
                qis = [self.sb(ph, [64, 16, 128], BF16, "qi") for _ in range(3)]
                accs = [self.sb(ph, [128, SEQ], F32, "acc") for _ in range(2 * G)]
                junks = [self.sb(ph, [128, SEQ], BF16, "sjunk") for _ in range(G)]
                rbuf = [self.sb(ph, [128, 512], F32, "rb") for _ in range(6)]
                sts = [self.sb(ph, [128, 8], F32, "bst") for _ in range(2 * G)]
                mqs = [self.sb(ph, [128, SEQ], BF16, "mq") for _ in range(2)]
                mts = [self.sb(ph, [128, NT, 128], BF16, "mts") for _ in range(2)]
                cnt = {"nr": 0, "nm": 0}
                groups = [list(range(g0, min(NT, g0 + G))) for g0 in range(0, NT, G)]

                def score_units(tiles):
                    units = []
                    for i in tiles:
                        qi, acc, stt = qis[i % 3], accs[i % (2 * G)], sts[i % (2 * G)]
                        Wd = (i + 1) * 128
                        first = [True]
                        for s0 in range(0, Wd, 512):
                            w = min(512, Wd - s0)
                            for hh in range(16):
                                def unit(i=i, qi=qi, acc=acc, s0=s0, w=w, hh=hh, ld=(s0 == 0 and hh == 0)):
                                    if ld:
                                        S.dma("sp", qi[:], R["q64T"][:, :, i * 128:(i + 1) * 128].rearrange("h d t -> d h t"), writes=[qi])
                                    pb = self.bank("g")
                                    S.op("pe", lambda h: h.matmul(
                                        pb[:, 0:w], qi[:, hh, :], kiT[0:64, s0:s0 + w], start=True, stop=True),
                                        reads=[qi, kiT], writes=[pb])
                                    rb = rbuf[cnt["nr"] % 6]
                                    cnt["nr"] += 1
                                    if hh % 4 != 3:
                                        S.op("act", lambda h: h.activation(
                                            rb[:, 0:w], pb[:, 0:w], AF.Relu, scale=aw[:, i, hh:hh + 1]), reads=[pb, aw], writes=[rb])
                                    else:
                                        S.op("dve", lambda h: h.tensor_scalar(
                                            rb[:, 0:w], pb[:, 0:w], aw[:, i, hh:hh + 1], 0.0, ALU.mult, ALU.max),
                                            reads=[pb, aw], writes=[rb])
                                    if hh == 0:
                                        S.op("dve", lambda h: h.tensor_scalar(
                                            acc[:, s0:s0 + w], rb[:, 0:w], sg[:, i, hh:hh + 1], None, ALU.mult),
                                            reads=[rb, sg], writes=[acc])
                                    else:
                                        S.op("dve", lambda h: h.scalar_tensor_tensor(
                                            acc[:, s0:s0 + w], rb[:, 0:w], sg[:, i, hh:hh + 1], acc[:, s0:s0 + w], ALU.mult, ALU.add),
                                            reads=[rb, sg, acc], writes=[acc])
                                units.append(unit)

                        def prep(i=i, acc=acc, stt=stt, Wd=Wd):
                            if i >= 2:
                                S.op("dve", lambda h: h.tensor_reduce(stt[:, 1:2], acc[:, 0:Wd], AX.X, ALU.max), reads=[acc], writes=[stt])
                                S.op("dve", lambda h: h.tensor_reduce(stt[:, 5:6], acc[:, 0:Wd], AX.X, ALU.min), reads=[acc], writes=[stt])
                                S.op("dve", lambda h: h.tensor_tensor(stt[:, 1:2], stt[:, 1:2], stt[:, 5:6], ALU.subtract), reads=[stt], writes=[stt])
                                S.op("dve", lambda h: h.tensor_scalar(stt[:, 0:1], stt[:, 5:6], -1.0, None, ALU.mult), reads=[stt], writes=[stt])
                            S.op("dve", lambda h: h.tensor_tensor(
                                acc[:, i * 128:(i + 1) * 128], acc[:, i * 128:(i + 1) * 128], pm[:], ALU.min), reads=[acc, pm], writes=[acc])
                        units.append(prep)
                    return units

                def bisect_steps(tiles):
                    steps = []
                    act_tiles = [i for i in tiles if i >= 2]
                    for k in range(NIT):
                        def issue(k=k):
                            step = 2.0 ** -(k + 1)
                            for i in act_tiles:
                                stt = sts[i % (2 * G)]
                                S.op("dve", lambda h, stt=stt: h.scalar_tensor_tensor(
                                    stt[:, 2:3], stt[:, 1:2], -step, stt[:, 0:1], ALU.mult, ALU.add), reads=[stt], writes=[stt])
                            for i in act_tiles:
                                acc, stt, jk = accs[i % (2 * G)], sts[i % (2 * G)], junks[i % G]
                                Wd = (i + 1) * 128
                                S.op("act", lambda h, acc=acc, stt=stt, jk=jk, Wd=Wd: h.activation(
                                    jk[:, 0:Wd], acc[:, 0:Wd], AF.Sign, bias=stt[:, 2:3], scale=1.0, accum_out=stt[:, 3:4]),
                                    reads=[acc, stt], writes=[jk, stt])

                        def update(k=k):
                            step = 2.0 ** -(k + 1)
                            for i in act_tiles:
                                stt = sts[i % (2 * G)]
                                Wd = (i + 1) * 128
                                S.op("dve", lambda h, stt=stt, Wd=Wd: h.tensor_scalar(
                                    stt[:, 4:5], stt[:, 3:4], float(512 - Wd), -step, ALU.is_ge, ALU.mult), reads=[stt], writes=[stt])
                                S.op("dve", lambda h, stt=stt: h.scalar_tensor_tensor(
                                    stt[:, 0:1], stt[:, 4:5], stt[:, 1:2], stt[:, 0:1], ALU.mult, ALU.add), reads=[stt], writes=[stt])
                        steps.append((issue, update))
                    return steps

                def emit_masks(tiles):
                    for i in tiles:
                        acc, stt = accs[i % (2 * G)], sts[i % (2 * G)]
                        mq, mt = mqs[cnt["nm"] % 2], mts[cnt["nm"] % 2]
                        cnt["nm"] += 1
                        Wd = (i + 1) * 128
                        if i >= 2:
                            S.op("dve", lambda h, stt=stt: h.tensor_scalar(
                                stt[:, 5:6], stt[:, 0:1], -1.0, None, ALU.mult), reads=[stt], writes=[stt])
                            S.op("dve", lambda h, acc=acc, mq=mq, Wd=Wd, stt=stt: h.tensor_scalar(
                                mq[:, 0:Wd], acc[:, 0:Wd], stt[:, 5:6], None, ALU.is_ge), reads=[acc, stt], writes=[mq])
                        else:
                            S.op("dve", lambda h, acc=acc, mq=mq, Wd=Wd: h.tensor_scalar(
                                mq[:, 0:Wd], acc[:, 0:Wd], -0.5 * BIG, None, ALU.is_gt), reads=[acc], writes=[mq])
                        for j0 in range(0, i + 1, 8):
                            nj = min(8, i + 1 - j0)
                            tb = self.bank("t")
                            tbv = tb.ap.bitcast(BF16)
                            for k in range(nj):
                                S.op("pe", lambda h, tbv=tbv, k=k, j0=j0, mq=mq: h.transpose(
                                    tbv[:, k * 128:(k + 1) * 128], mq[:, (j0 + k) * 128:(j0 + k + 1) * 128], C["ident"][:]),
                                    reads=[mq, C["ident"]], writes=[tb])
                            S.op("act", lambda h, tbv=tbv, nj=nj, j0=j0, mt=mt: h.copy(
                                mt[:, j0:j0 + nj, :], tbv[:, 0:nj * 128].rearrange("p (k t) -> p k t", k=nj)),
                                reads=[tb], writes=[mt])
                        jmax = 4 * (i // 4) + 3
                        if jmax > i:
                            S.op("pool", lambda h, mt=mt, i=i, jmax=jmax: h.memset(mt[:, i + 1:jmax + 1, :], 0.0), writes=[mt])
                        S.dma("sp", R["mT"][0:jmax + 1, :, i * 128:(i + 1) * 128].rearrange("j p q -> p j q"),
                              mt[:, 0:jmax + 1, :], reads=[mt], dram_write=True)

                for u in score_units(groups[0]):
                    u()
                for gi, tiles in enumerate(groups):
                    steps = bisect_steps(tiles)
                    nxt = score_units(groups[gi + 1]) if gi + 1 < len(groups) else []
                    per = (len(nxt) + NIT - 1) // NIT if nxt else 0
                    ui = 0
                    for k, (issue, update) in enumerate(steps):
                        issue()
                        for _ in range(per):
                            if ui < len(nxt):
                                nxt[ui]()
                                ui += 1
                        update()
                    while ui < len(nxt):
                        nxt[ui]()
                        ui += 1
                    emit_masks(tiles)
                S.barrier()

            gT = self.sb(lay, [128, 20, SEQ], BF16, "gT")
            gviews = [Tk(None, f"gv{i}") for i in range(20)]
            wo0 = self.sb(lay, [128, 20, 512], BF16, "wo0")
            S.dma("pool", wo0[:], I[f"l{l}_w_out"].rearrange("(kc p) c -> p kc c", p=128)[:, :, 0:512], writes=[wo0])
            with ExitStack() as ph:
                W = self.attn_work(ph, recip_act=True, skew=4)
                kT = self.sb(ph, [128, SEQ], BF16, "kTd")
                vv = self.sb(ph, [128, NT, 128], BF16, "vd")
                S.dma("sp", kT[:], R["kT"][0], writes=[kT])
                S.dma("sp", vv[:], R["v"][0].rearrange("(j p) d -> p j d", p=128), writes=[vv])
                mtc = [self.sb(ph, [128, 12, 512], BF16, "mtc"), self.sb(ph, [128, NT, 512], BF16, "mtc")]
                qcs = [self.sb(ph, [128, 512], BF16, "qc") for _ in range(3)]
                zcs = [self.sb(ph, [128, 512], BF16, "zc") for _ in range(3)]
                gcs = [self.sb(ph, [128, 512], BF16, "gc") for _ in range(3)]
                n = 0
                for c in range(NT // 4):
                    mc = mtc[c % 2]
                    nj = 4 * c + 4
                    S.dma("sp", mc[:, 0:nj, :], R["mT"][0:nj, :, c * 512:(c + 1) * 512].rearrange("j p q -> p j q"), writes=[mc])
                    for hh in range(16):
                        qc, zc, gc = qcs[n % 3], zcs[n % 3], gcs[n % 3]
                        n += 1
                        S.dma("sp", qc[:], R["qT"][hh][:, c * 512:(c + 1) * 512], writes=[qc])
                        S.dma("sp", zc[:], R["zT"][hh][:, c * 512:(c + 1) * 512], writes=[zc])
                        am = (lambda mc, c: (lambda j, iq: (mc[:, j, (iq - 4 * c) * 128:(iq - 4 * c + 1) * 128], mc)))(mc, c)
                        bm = (lambda mc: (lambda j, a, b: (mc[:, j, a * 128:b * 128], mc)))(mc)
                        sub = {"parts": [(kT.ap, kT, qc.ap, qc)], "v_ap": vv.ap, "v_tk": vv,
                               "blocks": self.causal_blocks(c, allmask=am), "qcol0": 0, "blockmask": bm}
                        self.attn_unit(W, 4 * c, 4, [sub], 128 ** -0.5, zc[:], zc,
                                       gT[:, hh, c * 512:(c + 1) * 512], gviews[hh], 0)
                self.attn_flush(W)
                S.barrier()
            with ExitStack() as ph:
                W = self.attn_work(ph, recip_act=True)
                self.mem_heads(ph, W, 16, l, gT=gT, gviews=gviews)
                S.barrier()
            self.out_proj(l, 20, xin, xout, gT=gT, gviews=gviews, wpre=wo0)


_CACHE = {}


def run(inputs, layers=(0, 1, 2, 3), final=True, cores=8, own3=False, r_override=None):
    import ml_dtypes
    key = (tuple(layers), final, own3)
    if key not in _CACHE:
        _CACHE[key] = Builder(layers, final, own3=own3).build()
    nc = _CACHE[key]
    consts = host_consts()
    shared = dict(consts)
    for k, v in inputs.items():
        if k in ("x", "mem", "positions"):
            continue
        a = np.ascontiguousarray(np.asarray(v))
        if a.ndim == 1:
            a = a.reshape(1, -1)
        shared[k] = a
    kp = np.arange(128)[:, None]
    qf = np.arange(128)[None, :]
    tri = (qf >= kp).astype(np.float32)
    in_maps = []
    for c in range(cores):
        b = c // 2 if cores == 8 else c
        r = c % 2 if cores == 8 else 0
        if r_override is not None:
            r = r_override
        m = dict(shared)
        m["x"] = np.ascontiguousarray(np.asarray(inputs["x"])[b])
        m["mem"] = np.ascontiguousarray(np.asarray(inputs["mem"])[b])
        m["pos"] = np.ascontiguousarray(np.asarray(inputs["positions"])[b].reshape(1, -1).astype(np.int32))
        sel = np.zeros((128, 2), np.float32)
        sel[:, r] = 1.0
        m["c_sel"] = sel
        own = np.concatenate([tri if r == 0 else np.ones_like(tri), np.zeros_like(tri) if r == 0 else tri], axis=1)
        m["c_own"] = own.astype(ml_dtypes.bfloat16)
        in_maps.append(m)
    res = run_bass_kernel_spmd(nc, in_maps, core_ids=list(range(cores)))
    return res


def kernel(**inputs):
    res = run(inputs, own3=True)
    B = 4
    out = np.empty((B, NT // 2, 2, 128, D), np.float32)
    for b in range(B):
        for r in range(2):
            out[b, :, r] = res.results[2 * b + r]["out"].reshape(NT // 2, 128, D)
    return out.reshape(B, SEQ, D)
```

```python
import math
from contextlib import ExitStack

import numpy as np
import concourse.bass as bass
import concourse.mybir as mybir
from concourse.bass_utils import run_bass_kernel_spmd

F32 = mybir.dt.float32
BF16 = mybir.dt.bfloat16
I32 = mybir.dt.int32
AF = mybir.ActivationFunctionType
ALU = mybir.AluOpType
AX = mybir.AxisListType

D = 2048
SEQ = 2048
NT = SEQ // 128
KC = D // 128
NMEM = 256
EPS = 1e-6
PI = math.pi

A_SIZES = (2048, 2048, 2048, 16, 512, 2560)
B_SIZES = (2048, 128, 128, 1024, 64, 16, 512, 2560)
C_SIZES = (2304, 2304, 2304, 512, 1280)
D_SIZES = (512, 512, 64, 512, 2560)


def offs(sizes):
    o = [0]
    for s in sizes:
        o.append(o[-1] + s)
    return o


class Tk:
    __slots__ = ("ap", "w", "r", "dsem", "name")

    def __init__(self, ap, name=""):
        self.ap = ap
        self.w = None
        self.r = {}
        self.dsem = None
        self.name = name

    def __getitem__(self, idx):
        return self.ap[idx]


class Sched:
    ENGS = ("pe", "act", "dve", "pool", "sp")

    def __init__(self, nc, stack):
        self.nc = nc
        self.stack = stack
        self.streams = {e: [] for e in self.ENGS}
        self.sems = {}
        self.count = {}
        self.seen = {e: {} for e in self.ENGS}
        self.nsem = 0
        for e in ("pe", "act", "dve", "pool"):
            self.esem(e)
        self.pending_st = {}
        self.ninstr = 0
        self.free_dsems = {"sw": [], "hw": []}

    def release(self, tk):
        if tk.dsem is not None:
            for cls, key in tk.dsem.items():
                self.free_dsems[cls].append(key)
            tk.dsem = None

    def esem(self, key):
        if key not in self.sems:
            self.sems[key] = self.stack.enter_context(self.nc.semaphore(f"s{self.nsem}"))
            self.nsem += 1
            self.count[key] = 0
        return self.sems[key]

    def _wait(self, eng, tok):
        if tok is None:
            return
        key, val = tok
        if self.seen[eng].get(key, 0) >= val:
            return
        if key == eng and eng == "pe":
            return
        self.seen[eng][key] = val
        sem = self.sems[key]
        self.streams[eng].append(lambda h, sem=sem, val=val: h.wait_ge(sem, val))
        self.ninstr += 1

    def _deps(self, eng, reads, writes):
        for t in reads:
            self._wait(eng, t.w)
        for t in writes:
            self._wait(eng, t.w)
            for k, v in t.r.items():
                self._wait(eng, (k, v))

    def op(self, eng, fn, reads=(), writes=()):
        self._deps(eng, reads, writes)
        self.count[eng] += 1
        n = self.count[eng]
        sem = self.sems[eng]
        self.streams[eng].append(lambda h, fn=fn, sem=sem: fn(h).then_inc(sem, 1))
        self.ninstr += 1
        for t in reads:
            if t.r.get(eng, 0) < n:
                t.r[eng] = n
        for t in writes:
            t.w = (eng, n)
            t.r = {}

    def dma(self, q, out, in_, reads=(), writes=(), dram_write=False):
        self._deps(q, reads, writes)
        t = writes[0] if writes else reads[0]
        cls = "sw" if q == "pool" else "hw"
        if t.dsem is None:
            t.dsem = {}
        if cls not in t.dsem:
            if self.free_dsems[cls]:
                t.dsem[cls] = self.free_dsems[cls].pop()
            else:
                t.dsem[cls] = f"d{self.nsem}"
                self.esem(t.dsem[cls])
        key = t.dsem[cls]
        self.count[key] += 16
        val = self.count[key]
        sem = self.sems[key]
        self.streams[q].append(
            lambda h, out=out, in_=in_, sem=sem: h.dma_start(out=out, in_=in_).then_inc(sem, 16))
        self.ninstr += 1
        for tt in writes:
            tt.w = (key, val)
            tt.r = {}
        for tt in reads:
            if tt.r.get(key, 0) < val:
                tt.r[key] = val
        if dram_write:
            self.pending_st[key] = val

    def barrier(self):
        toks = [(e, self.count[e]) for e in ("pe", "act", "dve", "pool") if self.count[e] > 0]
        toks += list(self.pending_st.items())
        for e in self.ENGS:
            for tok in toks:
                if tok[0] == e:
                    continue
                self._wait(e, tok)
        self.pending_st = {}

    def emit(self):
        nc = self.nc
        with nc.Block() as block:
            @block.tensor
            def _(h):
                for f in self.streams["pe"]:
                    f(h)

            @block.scalar
            def _(h):
                for f in self.streams["act"]:
                    f(h)

            @block.vector
            def _(h):
                for f in self.streams["dve"]:
                    f(h)

            @block.gpsimd
            def _(h):
                for f in self.streams["pool"]:
                    f(h)

            @block.sync
            def _(h):
                for f in self.streams["sp"]:
                    f(h)


MASK_NAMES = ("tri", "low", "m4", "m4tri", "m4low", "m16", "m16tri")
WIDE_NAMES = ("m4", "m16")
MASK_COLS = len(MASK_NAMES) * 128 + len(WIDE_NAMES) * 512


def host_consts():
    import ml_dtypes
    kp = np.arange(128)[:, None]
    qf = np.arange(128)[None, :]
    d = qf - kp
    masks = {
        "tri": d >= 0,
        "low": d <= 0,
        "m4": d % 4 == 0,
        "m4tri": (d % 4 == 0) & (d >= 0),
        "m4low": (d % 4 == 0) & (d <= 0),
        "m16": d % 16 == 0,
        "m16tri": (d % 16 == 0) & (d >= 0),
    }
    mk = np.stack([masks[n] for n in MASK_NAMES], axis=1).astype(np.float32)
    mk = mk.reshape(128, len(MASK_NAMES) * 128)
    wide = [np.tile(masks[n].astype(np.float32), (1, 4)) for n in WIDE_NAMES]
    mk = np.concatenate([mk] + wide, axis=1).astype(ml_dtypes.bfloat16)
    ident = np.eye(128, dtype=np.float32).astype(ml_dtypes.bfloat16)
    cf = np.zeros((128, 8), np.float32)
    j = np.arange(128)
    cf[:, 0] = np.power(np.float32(10000.0), -(j % 64).astype(np.float32) * np.float32(2.0) / np.float32(128))
    cf[:64, 1] = np.power(np.float32(10000.0), -(j[:64] % 32).astype(np.float32) * np.float32(2.0) / np.float32(64))
    cf[:, 2] = np.where(j < 64, -1.0, 1.0)
    cf[:64, 3] = np.where(j[:64] < 32, -1.0, 1.0)
    cf[:, 4] = -PI
    tri32 = (kp <= qf).astype(np.float32)
    ones32 = np.ones((128, 128), np.float32)
    return {"c_masks": mk, "c_ident": ident, "c_f": cf, "c_tri32": tri32, "c_ones32": ones32}


class Builder:
    def __init__(self, layers=(0, 1, 2, 3), final=True, debug=None, own3=False):
        self.own3 = own3 and (3 in layers) and layers[-1] == 3
        self.layers = layers
        self.final = final
        self.debug = debug
        self.uid = 0
        self.nc = bass.Bass("TRN2", target_bir_lowering=False)
        self.st = ExitStack()

    def name(self, base):
        self.uid += 1
        return f"{base}_{self.uid}"

    def sb(self, stack, shape, dt, name="t"):
        tk = Tk(stack.enter_context(self.nc.sbuf_tensor(self.name(name), list(shape), dt)), name)
        if stack is not self.st:
            stack.callback(self.S.release, tk)
        return tk

    def dram_in(self, name, shape, dt):
        return self.nc.dram_tensor(name, list(shape), dt, kind="ExternalInput").ap()

    def dram(self, name, shape, dt):
        return self.nc.dram_tensor(name, list(shape), dt).ap()

    def build(self):
        nc = self.nc
        st = self.st
        self.S = Sched(nc, st)
        S = self.S
        I = {}
        I["x"] = self.dram_in("x", [SEQ, D], F32)
        I["mem"] = self.dram_in("mem", [NMEM, D], F32)
        I["pos"] = self.dram_in("pos", [1, SEQ], I32)
        wshapes = {
            "l0_w_in": [D, 9232], "l1_w_in": [D, 6480], "l2_w_in": [D, 8704], "l3_w_in": [D, 4160],
            "l0_w_out": [2560, D], "l1_w_out": [2560, D], "l2_w_out": [1280, D], "l3_w_out": [2560, D],
            "l3_w_uq": [512, 3072], "l3_w_ukv": [512, 4096],
        }
        for l in range(4):
            wshapes[f"l{l}_w_mem_kv"] = [D, 1024]
            I[f"l{l}_norm"] = self.dram_in(f"l{l}_norm", [1, D], F32)
            I[f"l{l}_mem_norm"] = self.dram_in(f"l{l}_mem_norm", [1, D], F32)
        for k, shp in wshapes.items():
            I[k] = self.dram_in(k, shp, F32)
        I["l0_forget_bias"] = self.dram_in("l0_forget_bias", [1, 16], F32)
        I["l3_q_norm"] = self.dram_in("l3_q_norm", [1, 512], F32)
        I["l3_kv_norm"] = self.dram_in("l3_kv_norm", [1, 512], F32)
        I["final_norm"] = self.dram_in("final_norm", [1, D], F32)
        I["c_masks"] = self.dram_in("c_masks", [128, MASK_COLS], BF16)
        I["c_ident"] = self.dram_in("c_ident", [128, 128], BF16)
        I["c_f"] = self.dram_in("c_f", [128, 8], F32)
        I["c_tri32"] = self.dram_in("c_tri32", [128, 128], F32)
        I["c_ones32"] = self.dram_in("c_ones32", [128, 128], F32)
        self.I = I
        I["c_sel"] = self.dram_in("c_sel", [128, 2], F32)
        I["c_own"] = self.dram_in("c_own", [128, 256], BF16)
        self.out = nc.dram_tensor("out", [SEQ // 2 if self.own3 else SEQ, D], F32, kind="ExternalOutput").ap()

        R = {}
        R["x"] = self.dram("r_x", [SEQ, D], F32)
        R["qT"] = self.dram("r_qT", [18, 128, SEQ], BF16)
        R["kT"] = self.dram("r_kT", [18, 128, SEQ], BF16)
        R["v"] = self.dram("r_v", [18, SEQ, 128], BF16)
        R["zT"] = self.dram("r_zT", [20, 128, SEQ], BF16)
        R["qmT"] = self.dram("r_qmT", [4, 128, SEQ], BF16)
        R["kmT"] = self.dram("r_kmT", [4, 4, 128, NMEM], BF16)
        R["vm"] = self.dram("r_vm", [4, 4, NMEM, 128], BF16)
        R["gT"] = self.dram("r_gT", [20, 128, SEQ], BF16)
        R["q64T"] = self.dram("r_q64T", [16, 64, SEQ], BF16)
        R["mT"] = self.dram("r_mT", [NT, 128, SEQ], BF16)
        R["xo"] = self.dram("r_xo", [SEQ // 2, D], F32)
        self.R = R

        self.ps = [Tk(st.enter_context(nc.psum_tensor(f"ps{i}", [128, 512], F32)), f"ps{i}") for i in range(8)]
        self.rot = {}

        C = {}
        C["masks"] = self.sb(st, [128, MASK_COLS], BF16, "masks")
        C["ident"] = self.sb(st, [128, 128], BF16, "ident")
        C["cf"] = self.sb(st, [128, 8], F32, "cf")
        C["ones"] = self.sb(st, [128, 128], BF16, "ones")
        S.dma("sp", C["masks"][:], I["c_masks"], writes=[C["masks"]])
        S.dma("sp", C["ident"][:], I["c_ident"], writes=[C["ident"]])
        S.dma("sp", C["cf"][:], I["c_f"], writes=[C["cf"]])
        S.op("dve", lambda h: h.memset(C["ones"][:], 1.0), writes=[C["ones"]])
        C["sel"] = self.sb(st, [128, 2], F32, "sel")
        C["own"] = self.sb(st, [128, 256], BF16, "ownm")
        S.dma("sp", C["sel"][:], I["c_sel"], writes=[C["sel"]])
        S.dma("sp", C["own"][:], I["c_own"], writes=[C["own"]])
        self.C = C
        self.rope_alloc()

        self.mem_kv_all()
        xin = I["x"]
        for l in self.layers:
            if l == 3 and self.own3:
                self.layer3_own(xin, R["xo"])
                xin = R["xo"]
            else:
                getattr(self, f"layer{l}")(xin, R["x"])
                xin = R["x"]
        ntile_out = NT // 2 if self.own3 else NT
        if self.final:
            self.final_norm(xin, ntile_out)
        else:
            self.copy_out(xin, ntile_out)
        S.barrier()
        S.emit()
        return nc

    def bank(self, pool):
        pools = {"s": (0, 1, 2, 7), "o": (3, 4), "d": (5, 6), "x": (7,), "g": (0, 1, 2, 3), "t": (4, 5), "y": (6, 7)}
        ids = pools[pool]
        k = self.rot.get(pool, 0)
        self.rot[pool] = k + 1
        return self.ps[ids[k % len(ids)]]

    def mask(self, name):
        i = MASK_NAMES.index(name)
        return self.C["masks"][:, i * 128:(i + 1) * 128]

    def wide_mask(self, name, n):
        o = len(MASK_NAMES) * 128 + WIDE_NAMES.index(name) * 512
        return self.C["masks"][:, o:o + n * 128]

    def rope_alloc(self):
        C, st = self.C, self.st
        for nm in ("128", "64"):
            C["cos" + nm] = self.sb(st, [128, SEQ], F32, "cos" + nm)
            C["sin" + nm] = self.sb(st, [128, SEQ], F32, "sin" + nm)

    def rope_ops(self, ph):
        S, C = self.S, self.C
        posi = self.sb(ph, [128, SEQ], I32, "posi")
        posf = self.sb(ph, [128, SEQ], F32, "posf")
        ang = self.sb(ph, [128, SEQ], F32, "ang")
        tmp = self.sb(ph, [128, SEQ], F32, "tmp")
        tmp2 = self.sb(ph, [128, SEQ], F32, "tmp2")
        ops = []

        def add(eng, fn, reads, writes):
            ops.append(lambda: S.op(eng, fn, reads=reads, writes=writes))
        ops.append(lambda: S.dma("sp", posi[:], self.I["pos"].partition_broadcast(128), writes=[posi]))
        add("dve", lambda h: h.tensor_copy(posf[:], posi[:]), [posi], [posf])
        cf = C["cf"]
        for nm, col, sgn, npart in (("128", 0, 2, 128), ("64", 1, 3, 64)):
            cosT, sinT = C["cos" + nm], C["sin" + nm]
            pp = slice(0, npart)
            add("dve", lambda h, pp=pp, col=col: h.tensor_scalar(
                ang[pp, :], posf[pp, :], cf[pp, col:col + 1], None, ALU.mult), [posf, cf], [ang])
            for dst, shift in ((sinT, 0.0), (cosT, 0.5 * PI)):
                add("dve", lambda h, pp=pp, shift=shift: h.tensor_scalar(
                    tmp[pp, :], ang[pp, :], shift, 1.0 / (2 * PI), ALU.add, ALU.mult), [ang], [tmp])
                add("dve", lambda h, pp=pp: h.tensor_copy(posi[pp, :], tmp[pp, :]), [tmp], [posi])
                add("dve", lambda h, pp=pp: h.tensor_copy(tmp[pp, :], posi[pp, :]), [posi], [tmp])
                add("dve", lambda h, pp=pp, shift=shift: h.tensor_scalar(
                    tmp2[pp, :], ang[pp, :], shift, None, ALU.add), [ang], [tmp2])
                add("dve", lambda h, pp=pp: h.scalar_tensor_tensor(
                    tmp[pp, :], tmp[pp, :], -2 * PI, tmp2[pp, :], ALU.mult, ALU.add), [tmp, tmp2], [tmp])
                add("dve", lambda h, pp=pp: h.tensor_scalar(
                    tmp2[pp, :], tmp[pp, :], PI, -2 * PI, ALU.is_gt, ALU.mult), [tmp], [tmp2])
                add("dve", lambda h, pp=pp: h.tensor_tensor(
                    tmp[pp, :], tmp[pp, :], tmp2[pp, :], ALU.add), [tmp, tmp2], [tmp])
                add("dve", lambda h, pp=pp: h.tensor_scalar(
                    tmp[pp, :], tmp[pp, :], -PI, PI, ALU.max, ALU.min), [tmp], [tmp])
                add("act", lambda h, pp=pp, dst=dst: h.activation(
                    dst[pp, :], tmp[pp, :], AF.Sin), [tmp], [dst])
            add("dve", lambda h, pp=pp, sgn=sgn, sinT=sinT: h.tensor_scalar(
                sinT[pp, :], sinT[pp, :], cf[pp, sgn:sgn + 1], None, ALU.mult), [sinT, cf], [sinT])
        return ops

    def norm_T(self, ph, src, ntile, g_dram, hT_views, hT_ap, scr=None, nb=3):
        S, C = self.S, self.C
        if scr is None:
            scr = {}
        if "gB" not in scr:
            scr["gB"] = self.sb(ph, [128, D], F32, "gB")
            scr["xts"] = [self.sb(ph, [128, D], F32, "xt") for _ in range(3)]
            scr["hbs"] = [self.sb(ph, [128, D], BF16, "hb") for _ in range(nb)]
            scr["junk"] = self.sb(ph, [128, D], BF16, "junk")
            scr["sss"] = [self.sb(ph, [128, 2], F32, "ss") for _ in range(nb)]
        gB, xts, hbs, junk, sss = scr["gB"], scr["xts"], scr["hbs"], scr["junk"], scr["sss"]
        S.dma("sp", gB[:], g_dram.partition_broadcast(128), writes=[gB])
        def stage1(i):
            xt, hb, ss = xts[i % 3], hbs[i % len(hbs)], sss[i % len(sss)]
            S.dma("sp", xt[:], src[i * 128:(i + 1) * 128, :], writes=[xt])
            S.op("act", lambda h, xt=xt, ss=ss: h.activation(junk[:], xt[:], AF.Square, scale=D ** -0.5, accum_out=ss[:, 0:1]),
                 reads=[xt], writes=[junk, ss])
            S.op("dve", lambda h, ss=ss: h.tensor_scalar(ss[:, 1:2], ss[:, 0:1], EPS, None, ALU.add),
                 reads=[ss], writes=[ss])
            S.op("act", lambda h, ss=ss: h.activation(ss[:, 1:2], ss[:, 1:2], AF.Sqrt), reads=[ss], writes=[ss])
            S.op("dve", lambda h, ss=ss: h.reciprocal(ss[:, 1:2], ss[:, 1:2]), reads=[ss], writes=[ss])
            S.op("dve", lambda h, xt=xt, hb=hb, ss=ss: h.scalar_tensor_tensor(
                hb[:], xt[:], ss[:, 1:2], gB[:], ALU.mult, ALU.mult), reads=[xt, ss, gB], writes=[hb])

        def stage2(i):
            hb = hbs[i % len(hbs)]
            for half in range(2):
                pb = self.bank("t")
                pbv = pb.ap.bitcast(BF16)
                for k in range(8):
                    kc = half * 8 + k
                    S.op("pe", lambda h, pbv=pbv, k=k, kc=kc, hb=hb: h.transpose(
                        pbv[:, k * 128:(k + 1) * 128], hb[:, kc * 128:(kc + 1) * 128], C["ident"][:]),
                        reads=[hb, C["ident"]], writes=[pb])
                eng = "act" if half == 0 else "dve"
                dst = hT_ap[:, half * 8:(half + 1) * 8, i * 128:(i + 1) * 128]
                srcv = pbv.rearrange("p (k t) -> p k t", k=8)
                if eng == "act":
                    S.op("act", lambda h, dst=dst, srcv=srcv: h.copy(dst, srcv), reads=[pb], writes=[hT_views[i]])
                else:
                    S.op("dve", lambda h, dst=dst, srcv=srcv: h.tensor_copy(dst, srcv), reads=[pb], writes=[hT_views[i]])

        stage1(0)
        for i in range(ntile):
            if i + 1 < ntile:
                stage1(i + 1)
            stage2(i)

    def gemm(self, ph, src_ap, src_views, ntok, kcn, w_ap, jobs, nbuf=3, wbufs=None):
        S = self.S
        panels = []
        cur = None
        for jb in jobs:
            if "panel" in jb:
                if cur is not None and cur.get("key") == jb["panel"]:
                    cur["jobs"].append(jb)
                else:
                    cur = {"c0": jb["panel"][0], "c1": jb["panel"][1], "jobs": [jb], "key": jb["panel"]}
                    panels.append(cur)
                continue
            if cur is not None and "key" not in cur and jb["c0"] == cur["c1"] and jb["c0"] + jb["nc"] - cur["c0"] <= 512:
                cur["jobs"].append(jb)
                cur["c1"] = jb["c0"] + jb["nc"]
            else:
                cur = {"c0": jb["c0"], "c1": jb["c0"] + jb["nc"], "jobs": [jb]}
                panels.append(cur)
        if wbufs is None:
            wbufs = [self.sb(ph, [128, kcn, 512], BF16, "wp") for _ in range(nbuf)]
        nbuf = len(wbufs)
        wsrc = w_ap.rearrange("(kc p) c -> p kc c", p=128)
        tchunk = min(512, ntok)
        def load(pi):
            pn = panels[pi]
            wp = wbufs[pi % nbuf]
            cw = pn["c1"] - pn["c0"]
            S.dma("pool", wp[:, :, 0:cw], wsrc[:, :, pn["c0"]:pn["c1"]], writes=[wp])

        for pi in range(min(nbuf - 1, len(panels))):
            load(pi)
        for pi, pn in enumerate(panels):
            wp = wbufs[pi % nbuf]
            if pi + nbuf - 1 < len(panels):
                load(pi + nbuf - 1)
            for jb in pn["jobs"]:
                o = jb["c0"] - pn["c0"]
                n = jb["nc"]
                if jb["mode"] == "F":
                    for t0 in range(0, ntok, tchunk):
                        t1 = t0 + tchunk
                        pb = self.bank("g")
                        views = src_views[t0 // 128:(t1 + 127) // 128]
                        for kc in range(kcn):
                            S.op("pe", lambda h, pb=pb, wp=wp, kc=kc, o=o, n=n, t0=t0, t1=t1: h.matmul(
                                pb[0:n, 0:t1 - t0], wp[:, kc, o:o + n], src_ap[:, kc, t0:t1],
                                start=(kc == 0), stop=(kc == kcn - 1)), reads=[wp] + views, writes=[pb])
                        jb["post"](jb, t0, t1, pb)
                else:
                    for i in range(ntok // 128):
                        pb = self.bank("g")
                        rsel = jb.get("rsel")
                        for kc in range(kcn):
                            if rsel is None:
                                S.op("pe", lambda h, pb=pb, wp=wp, kc=kc, o=o, n=n, i=i: h.matmul(
                                    pb[:, 0:n], src_ap[:, kc, i * 128:(i + 1) * 128], wp[:, kc, o:o + n],
                                    start=(kc == 0), stop=(kc == kcn - 1)), reads=[wp, src_views[i]], writes=[pb])
                            else:
                                S.op("pe", lambda h, pb=pb, wp=wp, kc=kc, i=i, rsel=rsel, jb=jb: h.matmul(
                                    jb["osel"](pb), src_ap[:, kc, i * 128:(i + 1) * 128], rsel(wp, kc),
                                    start=(kc == 0), stop=(kc == kcn - 1)), reads=[wp, src_views[i]], writes=[pb])
                        jb["post"](jb, i, pb)
                    if "flush" in jb:
                        jb["flush"]()

    def stager(self, ph, nbuf=3, width=SEQ, dt=BF16, name="stg"):
        bufs = [self.sb(ph, [128, width], dt, name) for _ in range(nbuf)]
        state = {"i": 0}

        def nxt():
            b = bufs[state["i"] % nbuf]
            state["i"] += 1
            return b
        return nxt

    def shared(self, ph, rope=False):
        sh = {"stg": self.stager(ph)}
        if rope:
            sh["tmpA"] = [self.sb(ph, [128, 512], F32, "rtA") for _ in range(3)]
            sh["tmpB"] = [self.sb(ph, [128, 512], F32, "rtB") for _ in range(3)]
        return sh

    def post_F_store(self, ph, dst_of_job, kind, npart=128, stg=None, sh=None, tabs=None):
        S, C = self.S, self.C
        nxt = stg or sh["stg"]
        tmpA = sh.get("tmpA") if sh else None
        tmpB = sh.get("tmpB") if sh else None
        state = {"cur": None, "n": 0}

        def post(jb, t0, t1, pb):
            if t0 == 0:
                state["cur"] = nxt()
            stgt = state["cur"]
            pp = slice(0, npart)
            if kind == "copy":
                eng = "act" if (state["n"] % 2 == 0) else "dve"
                if eng == "act":
                    S.op("act", lambda h: h.copy(stgt[pp, t0:t1], pb[pp, 0:t1 - t0]), reads=[pb], writes=[stgt])
                else:
                    S.op("dve", lambda h: h.tensor_copy(stgt[pp, t0:t1], pb[pp, 0:t1 - t0]), reads=[pb], writes=[stgt])
            elif kind == "silu":
                S.op("act", lambda h: h.activation(stgt[pp, t0:t1], pb[pp, 0:t1 - t0], AF.Silu), reads=[pb], writes=[stgt])
            else:
                hf = npart // 2
                cosT, sinT = (C["cos128"], C["sin128"]) if kind == "rope128" else (C["cos64"], C["sin64"])
                if tabs is not None:
                    cosT, sinT = tabs
                ta, tb = tmpA[state["n"] % 3], tmpB[state["n"] % 3]
                w = t1 - t0
                S.op("dve", lambda h: h.tensor_tensor(ta[0:hf, 0:w], pb[hf:npart, 0:w], sinT[0:hf, t0:t1], ALU.mult),
                     reads=[pb, sinT], writes=[ta])
                S.op("dve", lambda h: h.tensor_tensor(ta[hf:npart, 0:w], pb[0:hf, 0:w], sinT[hf:npart, t0:t1], ALU.mult),
                     reads=[pb, sinT], writes=[ta])
                S.op("dve", lambda h: h.tensor_tensor(tb[pp, 0:w], pb[pp, 0:w], cosT[pp, t0:t1], ALU.mult),
                     reads=[pb, cosT], writes=[tb])
                eng = "dve" if state["n"] % 2 == 0 else "pool"
                S.op(eng, lambda h: h.tensor_tensor(stgt[pp, t0:t1], ta[pp, 0:w], tb[pp, 0:w], ALU.add),
                     reads=[ta, tb], writes=[stgt])
            state["n"] += 1
            if t1 == SEQ or (jb.get("ntok") and t1 == jb["ntok"]):
                dst = dst_of_job(jb)
                S.dma("sp", dst, stgt[pp, 0:t1], reads=[stgt], dram_write=True)
        return post

    def post_T_store(self, ph, dst_fn, stg_width=512, stg=None):
        S = self.S
        nxt = stg or self.stager(ph, nbuf=3, width=stg_width, name="stgT")
        state = {"n": 0}

        def post(jb, i, pb):
            stgt = nxt()
            n = jb["nc"]
            if state["n"] % 2 == 0:
                S.op("act", lambda h: h.copy(stgt[:, 0:n], pb[:, 0:n]), reads=[pb], writes=[stgt])
            else:
                S.op("dve", lambda h: h.tensor_copy(stgt[:, 0:n], pb[:, 0:n]), reads=[pb], writes=[stgt])
            state["n"] += 1
            dst, srcv = dst_fn(jb, i, stgt)
            S.dma("sp", dst, srcv, reads=[stgt], dram_write=True)
        return post

    def mem_kv_all(self):
        S, R, I = self.S, self.R, self.I
        with ExitStack() as ph:
            scr = {}
            rops = self.rope_ops(ph)
            per = (len(rops) + len(self.layers) - 1) // len(self.layers)
            pF_stg = self.stager(ph, nbuf=2, width=NMEM, name="stgm")
            mTs = {l: self.sb(ph, [128, KC, NMEM], BF16, "memT") for l in self.layers}
            wbm = [self.sb(ph, [128, KC, 512], BF16, "wpm") for _ in range(2)]
            pT_st = self.stager(ph, nbuf=3, width=512, name="stgT")
            pT = None
            for l in self.layers:
                mT = mTs[l]
                views = [Tk(None, "memv0"), Tk(None, "memv1")]
                self.norm_T(ph, I["mem"], 2, I[f"l{l}_mem_norm"], views, mT.ap, scr=scr)
                for _ in range(per):
                    if rops:
                        rops.pop(0)()
                jobs = []
                pF = self.post_F_store(ph, (lambda l: (lambda jb: R["kmT"][l][jb["h"]]))(l), "copy", stg=pF_stg)
                for hh in range(4):
                    jobs.append({"mode": "F", "c0": hh * 128, "nc": 128, "h": hh, "post": pF, "ntok": NMEM})

                def dstT(jb, i, stgt, l=l):
                    return (R["vm"][l][:, i * 128:(i + 1) * 128, :].rearrange("h p d -> p h d"),
                            stgt[:, 0:512].rearrange("p (h d) -> p h d", h=4))
                jobs.append({"mode": "T", "c0": 512, "nc": 512, "post": self.post_T_store(ph, dstT, stg=pT_st)})
                pT = True
                self.gemm(ph, mT.ap, views, NMEM, KC, I[f"l{l}_w_mem_kv"], jobs, nbuf=2, wbufs=wbm)
            while rops:
                rops.pop(0)()
            S.barrier()

    def attn_unit(self, W, q0t, nqt, subs, scale, zs_ap, zs_tk, g_out, g_tk, col0, after=None):
        S, C = self.S, self.C
        wq = nqt * 128
        O = self.bank("o")
        Dn = self.bank("d")
        allb = [(sub, blk) for sub in subs for blk in sub["blocks"]]
        nb = len(allb)
        pend = W["pend"]
        for bi, (sub, (j, a, b, masks, biasf)) in enumerate(allb):
            Sp = self.bank("s")
            c0, c1 = a * 128, b * 128
            np_ = len(sub["parts"])
            for pi, (kT, kTk, qT, qTk) in enumerate(sub["parts"]):
                qc = sub["qcol0"]
                S.op("pe", lambda h, Sp=Sp, kT=kT, qT=qT, j=j, c0=c0, c1=c1, qc=qc, pi=pi, np_=np_: h.matmul(
                    Sp[:, c0:c1], kT[:, j * 128:(j + 1) * 128], qT[:, qc + c0:qc + c1],
                    start=(pi == 0), stop=(pi == np_ - 1)), reads=[kTk, qTk], writes=[Sp])
            PT = W["pt"][W["n"] % len(W["pt"])]
            W["n"] += 1
            if biasf is None:
                S.op("act", lambda h, PT=PT, Sp=Sp, c0=c0, c1=c1: h.activation(
                    PT[:, c0:c1], Sp[:, c0:c1], AF.Exp, scale=scale), reads=[Sp], writes=[PT])
            else:
                for u in range(a // 2, (b + 1) // 2):
                    ta, tb_ = max(a, 2 * u), min(b, 2 * u + 2)
                    bap, btk = biasf((q0t + 2 * u) // 2, j)
                    S.op("act", lambda h, PT=PT, Sp=Sp, ta=ta, tb_=tb_, bap=bap: h.activation(
                        PT[:, ta * 128:tb_ * 128], Sp[:, ta * 128:tb_ * 128], AF.Exp, bias=bap, scale=scale),
                        reads=[Sp, btk], writes=[PT])
            if sub.get("blockmask") is not None:
                map_, mtk = sub["blockmask"](j, a, b)
                meng = "dve"
                S.op(meng, lambda h, PT=PT, c0=c0, c1=c1, map_=map_: h.tensor_tensor(
                    PT[:, c0:c1], PT[:, c0:c1], map_, ALU.mult), reads=[PT, mtk], writes=[PT])
            else:
                qt = a
                while qt < b:
                    m = masks[qt - a]
                    if m is None:
                        qt += 1
                        continue
                    map_, mtk = m[0], m[1]
                    n = 1
                    if len(m) > 2 and m[2] in WIDE_NAMES:
                        while qt + n < b and masks[qt + n - a] is not None and len(masks[qt + n - a]) > 2 \
                                and masks[qt + n - a][2] == m[2]:
                            n += 1
                        if n > 1:
                            map_ = self.wide_mask(m[2], n)
                    S.op("dve", lambda h, PT=PT, qt=qt, n=n, map_=map_: h.tensor_tensor(
                        PT[:, qt * 128:(qt + n) * 128], PT[:, qt * 128:(qt + n) * 128], map_, ALU.mult),
                        reads=[PT, mtk], writes=[PT])
                    qt += n
            vap, vtk = sub["v_ap"], sub["v_tk"]

            def pv(O=O, Dn=Dn, vap=vap, vtk=vtk, j=j, PT=PT, c0=c0, c1=c1, bi=bi):
                S.op("pe", lambda h: h.matmul(
                    O[:, c0:c1], vap[:, j, :], PT[:, c0:c1], start=(bi == 0), stop=(bi == nb - 1)),
                    reads=[vtk, PT], writes=[O])
                S.op("pe", lambda h: h.matmul(
                    Dn[:, c0:c1], C["ones"][:], PT[:, c0:c1], start=(bi == 0), stop=(bi == nb - 1)),
                    reads=[C["ones"], PT], writes=[Dn])
            pend.append(pv)
            while len(pend) > W["skew"]:
                pend.pop(0)()
        rd = W["rd"][W["m"] % 2]
        ob = W["ob"][W["m"] % 2]
        W["m"] += 1

        def fin(O=O, Dn=Dn, rd=rd, ob=ob):
            if W.get("recip_act"):
                S.op("act", lambda h: h.activation(rd[:, 0:wq], Dn[:, 0:wq], AF.Ln), reads=[Dn], writes=[rd])
                S.op("act", lambda h: h.activation(rd[:, 0:wq], rd[:, 0:wq], AF.Exp, scale=-1.0), reads=[rd], writes=[rd])
            else:
                S.op("dve", lambda h: h.reciprocal(rd[:, 0:wq], Dn[:, 0:wq]), reads=[Dn], writes=[rd])
            S.op("dve", lambda h: h.tensor_tensor(ob[:, 0:wq], O[:, 0:wq], rd[:, 0:wq], ALU.mult), reads=[O, rd], writes=[ob])
            S.op("pool", lambda h: h.tensor_tensor(g_out, ob[:, 0:wq], zs_ap, ALU.mult), reads=[ob, zs_tk], writes=[g_tk])
            if after is not None:
                after()
        pend.append(fin)

    def attn_flush(self, W):
        while W["pend"]:
            W["pend"].pop(0)()

    def attn_work(self, ph, recip_act=False, skew=3):
        return {"recip_act": recip_act,
                "pt": [self.sb(ph, [128, 512], BF16, "pt") for _ in range(2 * skew)],
                "rd": [self.sb(ph, [128, 512], F32, "rd") for _ in range(2)],
                "ob": [self.sb(ph, [128, 512], F32, "ob") for _ in range(2)],
                "n": 0, "m": 0, "pend": [], "skew": skew}

    def causal_blocks(self, c, biasf=None, allmask=None):
        blocks = []
        for j in range(4 * c + 4):
            a = max(0, j - 4 * c)
            masks = []
            for qt in range(a, 4):
                if allmask is not None:
                    masks.append(allmask(j, 4 * c + qt))
                elif 4 * c + qt == j:
                    masks.append((self.mask("tri"), self.C["masks"]))
                else:
                    masks.append(None)
            blocks.append((j, a, 4, masks, biasf))
        return blocks

    def attn_heads(self, ph, W, heads, scale, gchunk0, load_q64=None, nq=SEQ, gT=None, gviews=None):
        S, R = self.S, self.R
        nbuf = 2
        nsub = max(len(hd) for hd in heads)
        qb = [[self.sb(ph, [128, SEQ], BF16, "qb") for _ in range(nsub)] for _ in range(nbuf)]
        kb = [[self.sb(ph, [128, SEQ], BF16, "kb") for _ in range(nsub)] for _ in range(nbuf)]
        vb = [[self.sb(ph, [128, NT, 128], BF16, "vb") for _ in range(nsub)] for _ in range(nbuf)]
        zb = [self.sb(ph, [128, SEQ], BF16, "zb") for _ in range(nbuf)]
        gb = [self.sb(ph, [128, SEQ], BF16, "gb") for _ in range(nbuf)]
        q2 = k2 = None
        if load_q64 is not None:
            q2 = [self.sb(ph, [64, SEQ], BF16, "q2") for _ in range(nbuf)]
        for n, hd in enumerate(heads):
            bsel = n % nbuf
            subs = []
            for si, sp in enumerate(hd):
                q, k, v = qb[bsel][si], kb[bsel][si], vb[bsel][si]
                nk = sp.get("nk", NT)
                S.dma("sp", q[:, 0:nq], sp["q"][:, 0:nq], writes=[q])
                S.dma("sp", k[:, 0:nk * 128], sp["k"], writes=[k])
                S.dma("sp", v[:, 0:nk, :], sp["v"].rearrange("(j p) d -> p j d", p=128), writes=[v])
                parts = [(k.ap, k, q.ap, q)]
                if load_q64 is not None and "q64" in sp:
                    S.dma("sp", q2[bsel][:, 0:nq], sp["q64"][:, 0:nq], writes=[q2[bsel]])
                    k64 = load_q64
                    parts.append((k64.ap, k64, q2[bsel].ap, q2[bsel]))
                subs.append((sp, parts, v))
            z, g = zb[bsel], gb[bsel]
            S.dma("sp", z[:, 0:nq], R["zT"][gchunk0 + n][:, 0:nq], writes=[z])
            nch = nq // 512
            for c in range(nch):
                ss = []
                for sp, parts, v in subs:
                    ss.append({"parts": parts, "v_ap": v.ap, "v_tk": v, "blocks": sp["blocks_fn"](c), "qcol0": c * 512})
                aft = None
                if gT is not None:
                    g_ap, g_tk = gT[:, gchunk0 + n, c * 512:(c + 1) * 512], gviews[gchunk0 + n]
                else:
                    g_ap, g_tk = g[:, c * 512:(c + 1) * 512], g
                    if c == nch - 1:
                        aft = (lambda g=g, n=n: S.dma("pool", R["gT"][gchunk0 + n][:, 0:nq], g[:, 0:nq], reads=[g], dram_write=True))
                self.attn_unit(W, 4 * c, 4, ss, hd[0].get("scale", scale), z[:, c * 512:(c + 1) * 512], z,
                               g_ap, g_tk, c * 512, after=aft)
        self.attn_flush(W)

    def mem_head_specs(self, l):
        R = self.R
        heads = []
        for hh in range(4):
            heads.append([{"q": R["qmT"][hh], "k": R["kmT"][l][hh], "v": R["vm"][l][hh], "nk": 2, "scale": 128 ** -0.5,
                           "blocks_fn": lambda c: [(0, 0, 4, [None] * 4, None), (1, 0, 4, [None] * 4, None)]}])
        return heads

    def mem_heads(self, ph, W, gchunk0, l, nq=SEQ, gT=None, gviews=None):
        self.attn_heads(ph, W, self.mem_head_specs(l), 128 ** -0.5, gchunk0, nq=nq, gT=gT, gviews=gviews)

    def out_proj(self, l, nchunk, xin, xout, gT=None, gviews=None, wpre=None):
        S, R, I = self.S, self.R, self.I
        with ExitStack() as ph:
            if gT is None:
                gT = self.sb(ph, [128, nchunk, SEQ], BF16, "gT")
                for cc in range(nchunk):
                    S.dma("sp", gT[:, cc, :], R["gT"][cc], writes=[gT])
                gviews = [gT]
            xb = [self.sb(ph, [128, 512], F32, "xb") for _ in range(3)]
            if wpre is not None:
                wbufs = [wpre, self.sb(ph, [128, nchunk, 512], BF16, "wo")]
            else:
                wbufs = [self.sb(ph, [128, nchunk, 512], BF16, "wo") for _ in range(2)]
            wsrc = I[f"l{l}_w_out"].rearrange("(kc p) c -> p kc c", p=128)
            n = 0
            for pc in range(4):
                wp = wbufs[pc % 2]
                if not (pc == 0 and wpre is not None):
                    S.dma("pool", wp[:], wsrc[:, :, pc * 512:(pc + 1) * 512], writes=[wp])
                for i in range(NT):
                    xt = xb[n % 3]
                    n += 1
                    S.dma("act", xt[:], xin[i * 128:(i + 1) * 128, pc * 512:(pc + 1) * 512], writes=[xt])
                    pb = self.bank("g")
                    for cc in range(nchunk):
                        S.op("pe", lambda h, pb=pb, cc=cc, i=i, wp=wp: h.matmul(
                            pb[:], gT[:, cc, i * 128:(i + 1) * 128], wp[:, cc, :],
                            start=(cc == 0), stop=(cc == nchunk - 1)), reads=[gviews[cc % len(gviews)], wp], writes=[pb])
                    S.op("dve", lambda h, xt=xt, pb=pb: h.tensor_tensor(xt[:], xt[:], pb[:], ALU.add),
                         reads=[xt, pb], writes=[xt])
                    S.dma("sp", xout[i * 128:(i + 1) * 128, pc * 512:(pc + 1) * 512], xt[:], reads=[xt], dram_write=True)
            S.barrier()

    def final_norm(self, xin, ntile=NT):
        S, I = self.S, self.I
        with ExitStack() as ph:
            gB = self.sb(ph, [128, D], F32, "gBf")
            S.dma("sp", gB[:], I["final_norm"].partition_broadcast(128), writes=[gB])
            xts = [self.sb(ph, [128, D], F32, "xtf") for _ in range(3)]
            junk = self.sb(ph, [128, D], BF16, "junkf")
            sss = [self.sb(ph, [128, 2], F32, "ssf") for _ in range(3)]
            for i in range(ntile):
                xt, ss = xts[i % 3], sss[i % 3]
                S.dma("sp", xt[:], xin[i * 128:(i + 1) * 128, :], writes=[xt])
                S.op("act", lambda h, xt=xt, ss=ss: h.activation(junk[:], xt[:], AF.Square, scale=D ** -0.5, accum_out=ss[:, 0:1]),
                     reads=[xt], writes=[junk, ss])
                S.op("dve", lambda h, ss=ss: h.tensor_scalar(ss[:, 1:2], ss[:, 0:1], EPS, None, ALU.add),
                     reads=[ss], writes=[ss])
                S.op("act", lambda h, ss=ss: h.activation(ss[:, 1:2], ss[:, 1:2], AF.Sqrt), reads=[ss], writes=[ss])
                S.op("dve", lambda h, ss=ss: h.reciprocal(ss[:, 1:2], ss[:, 1:2]), reads=[ss], writes=[ss])
                S.op("dve", lambda h, xt=xt, ss=ss: h.scalar_tensor_tensor(
                    xt[:], xt[:], ss[:, 1:2], gB[:], ALU.mult, ALU.mult), reads=[xt, ss, gB], writes=[xt])
                S.dma("sp", self.out[i * 128:(i + 1) * 128, :], xt[:], reads=[xt], dram_write=True)

    def copy_out(self, xin, ntile=NT):
        S = self.S
        with ExitStack() as ph:
            xts = [self.sb(ph, [128, D], F32, "xtc") for _ in range(2)]
            for i in range(ntile):
                xt = xts[i % 2]
                S.dma("sp", xt[:], xin[i * 128:(i + 1) * 128, :], writes=[xt])
                S.dma("sp", self.out[i * 128:(i + 1) * 128, :], xt[:], reads=[xt], dram_write=True)

    def project(self, l, xin, make_jobs, extra=None):
        S, I = self.S, self.I
        with ExitStack() as ph:
            hT = self.sb(ph, [128, KC, SEQ], BF16, "hT")
            views = [Tk(None, f"hv{i}") for i in range(NT)]
            with ExitStack() as pa:
                self.norm_T(pa, xin, NT, I[f"l{l}_norm"], views, hT.ap)
                S.barrier()
            jobs = make_jobs(ph)
            self.gemm(ph, hT.ap, views, SEQ, KC, I[f"l{l}_w_in"], jobs)
            if extra is not None:
                extra(ph)
            S.barrier()

    def std_jobs(self, ph, sh, o_qm, o_z, nz):
        R = self.R
        jobs = []
        pqm = self.post_F_store(ph, lambda jb: R["qmT"][jb["h"]], "copy", sh=sh)
        for hh in range(4):
            jobs.append({"mode": "F", "c0": o_qm + hh * 128, "nc": 128, "h": hh, "post": pqm})
        pz = self.post_F_store(ph, lambda jb: R["zT"][jb["h"]], "silu", sh=sh)
        for cc in range(nz):
            jobs.append({"mode": "F", "c0": o_z + cc * 128, "nc": 128, "h": cc, "post": pz})
        return jobs

    def v_jobs(self, ph, o_v, nheads):
        R = self.R

        def dstT(jb, i, stgt):
            nh = jb["nc"] // 128
            h0 = jb["h0"]
            return (R["v"][h0:h0 + nh, i * 128:(i + 1) * 128, :].rearrange("h p d -> p h d"),
                    stgt[:, 0:jb["nc"]].rearrange("p (h d) -> p h d", h=nh))
        pv = self.post_T_store(ph, dstT)
        jobs = []
        h0 = 0
        while h0 < nheads:
            nh = min(4, nheads - h0)
            jobs.append({"mode": "T", "c0": o_v + h0 * 128, "nc": nh * 128, "h0": h0, "post": pv})
            h0 += nh
        return jobs

    def layer3(self, xin, xout):
        S, R, I, C = self.S, self.R, self.I, self.C
        o = offs(D_SIZES)
        l = 3
        with ExitStack() as lay:
            cqT = self.sb(lay, [128, 4, SEQ], BF16, "cqT")
            ckvT = self.sb(lay, [128, 4, SEQ], BF16, "ckvT")
            krT = self.sb(lay, [64, SEQ], BF16, "krT")
            cq_views = [Tk(None, f"cqv{i}") for i in range(NT)]
            ckv_views = [Tk(None, f"ckvv{i}") for i in range(NT)]

            def make_jobs(ph):
                jobs = []
                gq = self.sb(ph, [128, 512], F32, "gq")
                gkv = self.sb(ph, [128, 512], F32, "gkv")
                S.dma("sp", gq[:], I["l3_q_norm"].partition_broadcast(128), writes=[gq])
                S.dma("sp", gkv[:], I["l3_kv_norm"].partition_broadcast(128), writes=[gkv])
                junk = self.sb(ph, [128, 512], BF16, "junkl")
                sss = [self.sb(ph, [128, 2], F32, "ssl") for _ in range(2)]
                cbs = [self.sb(ph, [128, 512], BF16, "cb") for _ in range(2)]
                st_ = {"n": 0}

                def mk_post(g, dstT, dviews):
                    def post(jb, i, pb):
                        ss, cb = sss[st_["n"] % 2], cbs[st_["n"] % 2]
                        st_["n"] += 1
                        S.op("act", lambda h: h.activation(junk[:], pb[:, 0:512], AF.Square, scale=512 ** -0.5, accum_out=ss[:, 0:1]),
                             reads=[pb], writes=[junk, ss])
                        S.op("dve", lambda h: h.tensor_scalar(ss[:, 1:2], ss[:, 0:1], EPS, None, ALU.add),
                             reads=[ss], writes=[ss])
                        S.op("act", lambda h: h.activation(ss[:, 1:2], ss[:, 1:2], AF.Sqrt), reads=[ss], writes=[ss])
                        S.op("dve", lambda h: h.reciprocal(ss[:, 1:2], ss[:, 1:2]), reads=[ss], writes=[ss])
                        S.op("dve", lambda h: h.scalar_tensor_tensor(
                            cb[:], pb[:, 0:512], ss[:, 1:2], g[:], ALU.mult, ALU.mult), reads=[pb, ss, g], writes=[cb])
                        tb = self.bank("t")
                        tbv = tb.ap.bitcast(BF16)
                        for k in range(4):
                            S.op("pe", lambda h, k=k: h.transpose(
                                tbv[:, k * 128:(k + 1) * 128], cb[:, k * 128:(k + 1) * 128], C["ident"][:]),
                                reads=[cb, C["ident"]], writes=[tb])
                        S.op("act", lambda h: h.copy(dstT[:, :, i * 128:(i + 1) * 128],
                                                     tbv[:, 0:512].rearrange("p (k t) -> p k t", k=4)),
                             reads=[tb], writes=[dviews[i]])
                    return post
                jobs.append({"mode": "T", "c0": o[0], "nc": 512, "post": mk_post(gq, cqT, cq_views)})
                jobs.append({"mode": "T", "c0": o[1], "nc": 512, "post": mk_post(gkv, ckvT, ckv_views)})
                tmpA = self.sb(ph, [128, 512], F32, "krA")
                tmpB = self.sb(ph, [128, 512], F32, "krB")

                def post_kr(jb, t0, t1, pb):
                    self.rope_block(pb, 64, t0, t1, tmpA, tmpB, krT[0:64, t0:t1], krT)
                jobs.append({"mode": "F", "c0": o[2], "nc": 64, "post": post_kr})
                sh = self.shared(ph)
                jobs += self.std_jobs(ph, sh, o[3], o[4], 20)
                return jobs

            self.project(l, xin, make_jobs)

            with ExitStack() as ph:
                jobs = []
                sh = self.shared(ph, rope=True)
                pq = self.post_F_store(ph, lambda jb: R["qT"][jb["h"]], "copy", sh=sh)
                pqr = self.post_F_store(ph, lambda jb: R["q64T"][jb["h"]], "rope64", npart=64, sh=sh)
                for hh in range(16):
                    jobs.append({"mode": "F", "c0": hh * 192, "nc": 128, "h": hh, "post": pq})
                    jobs.append({"mode": "F", "c0": hh * 192 + 128, "nc": 64, "h": hh, "post": pqr})
                self.gemm(ph, cqT.ap, cq_views, SEQ, 4, I["l3_w_uq"], jobs)
                S.barrier()
            with ExitStack() as ph:
                jobs = []
                sh = self.shared(ph)
                pk = self.post_F_store(ph, lambda jb: R["kT"][jb["h"]], "copy", sh=sh)

                def dstT(jb, i, stgt):
                    return (R["v"][jb["h"], i * 128:(i + 1) * 128, :], stgt[:, 0:128])
                pv = self.post_T_store(ph, dstT)
                for hh in range(16):
                    jobs.append({"mode": "F", "c0": hh * 256, "nc": 128, "h": hh, "post": pk})
                    jobs.append({"mode": "T", "c0": hh * 256 + 128, "nc": 128, "h": hh, "post": pv})
                self.gemm(ph, ckvT.ap, ckv_views, SEQ, 4, I["l3_w_ukv"], jobs)
                S.barrier()

            with ExitStack() as ph:
                W = self.attn_work(ph)
                heads = []
                for hh in range(16):
                    heads.append([{"q": R["qT"][hh], "k": R["kT"][hh], "v": R["v"][hh], "q64": R["q64T"][hh],
                                   "blocks_fn": lambda c: self.causal_blocks(c)}])
                self.attn_heads(ph, W, heads, 192 ** -0.5, 0, load_q64=krT)
                S.barrier()
            with ExitStack() as ph:
                W = self.attn_work(ph)
                self.mem_heads(ph, W, 16, l)
                S.barrier()
        self.out_proj(l, 20, xin, xout)

    def own_blocks(self, c):
        ownm = self.C["own"]
        blocks = []
        for pk in range(0, 2 * (4 * c + 3) + 2):
            a = max(0, pk // 2 - 4 * c)
            masks = []
            for qt in range(a, 4):
                m = 4 * c + qt
                if pk == 2 * m:
                    masks.append((ownm[:, 0:128], ownm))
                elif pk == 2 * m + 1:
                    masks.append((ownm[:, 128:256], ownm))
                else:
                    masks.append(None)
            blocks.append((pk, a, 4, masks, None))
        return blocks

    def blend_tiles(self, dst_fn, src_fn, dst_tk, src_tk, nt=NT // 2):
        S, sel = self.S, self.C["sel"]
        for m in range(nt):
            np_ = dst_fn(m).shape[0]
            S.op("dve", lambda h, m=m, np_=np_: h.tensor_scalar(
                dst_fn(m), src_fn(2 * m), sel[0:np_, 0:1], None, ALU.mult), reads=[src_tk, sel], writes=[dst_tk])
            S.op("dve", lambda h, m=m, np_=np_: h.scalar_tensor_tensor(
                dst_fn(m), src_fn(2 * m + 1), sel[0:np_, 1:2], dst_fn(m), ALU.mult, ALU.add),
                reads=[src_tk, sel, dst_tk], writes=[dst_tk])

    def layer3_own(self, xin, xout):
        S, R, I, C = self.S, self.R, self.I, self.C
        o = offs(D_SIZES)
        l = 3
        HS = SEQ // 2
        with ExitStack() as lay:
            cqT = self.sb(lay, [128, 4, HS], BF16, "cqT")
            ckvT = self.sb(lay, [128, 4, SEQ], BF16, "ckvT")
            krT = self.sb(lay, [64, SEQ], BF16, "krT")
            cq_views = [Tk(None, f"cqv{i}") for i in range(NT // 2)]
            ckv_views = [Tk(None, f"ckvv{i}") for i in range(NT)]

            def latent_post(ph, g):
                junk = self.sb(ph, [128, 512], BF16, "junkl")
                sss = [self.sb(ph, [128, 2], F32, "ssl") for _ in range(2)]
                cbs = [self.sb(ph, [128, 512], BF16, "cb") for _ in range(2)]
                st_ = {"n": 0}

                def mk_post(dstT, dviews):
                    def post(jb, i, pb):
                        ss, cb = sss[st_["n"] % 2], cbs[st_["n"] % 2]
                        st_["n"] += 1
                        S.op("act", lambda h: h.activation(junk[:], pb[:, 0:512], AF.Square, scale=512 ** -0.5, accum_out=ss[:, 0:1]),
                             reads=[pb], writes=[junk, ss])
                        S.op("dve", lambda h: h.tensor_scalar(ss[:, 1:2], ss[:, 0:1], EPS, None, ALU.add),
                             reads=[ss], writes=[ss])
                        S.op("act", lambda h: h.activation(ss[:, 1:2], ss[:, 1:2], AF.Sqrt), reads=[ss], writes=[ss])
                        S.op("dve", lambda h: h.reciprocal(ss[:, 1:2], ss[:, 1:2]), reads=[ss], writes=[ss])
                        S.op("dve", lambda h: h.scalar_tensor_tensor(
                            cb[:], pb[:, 0:512], ss[:, 1:2], g[:], ALU.mult, ALU.mult), reads=[pb, ss, g], writes=[cb])
                        def stage_b(i=i, cb=cb):
                            tb = self.bank("t")
                            tbv = tb.ap.bitcast(BF16)
                            for k in range(4):
                                S.op("pe", lambda h, k=k: h.transpose(
                                    tbv[:, k * 128:(k + 1) * 128], cb[:, k * 128:(k + 1) * 128], C["ident"][:]),
                                    reads=[cb, C["ident"]], writes=[tb])
                            S.op("act", lambda h: h.copy(dstT[:, :, i * 128:(i + 1) * 128],
                                                         tbv[:, 0:512].rearrange("p (k t) -> p k t", k=4)),
                                 reads=[tb], writes=[dviews[i]])
                        prev = pend_b.pop() if pend_b else None
                        pend_b.append(stage_b)
                        if prev is not None:
                            prev()

                    def flush():
                        while pend_b:
                            pend_b.pop()()
                    post.flush = flush
                    return post
                pend_b = []
                return mk_post

            with ExitStack() as pho:
                hTo = self.sb(pho, [128, KC, HS], BF16, "hTo")
                oviews = [hTo for _ in range(NT // 2)]
                with ExitStack() as ph:
                    hT = self.sb(ph, [128, KC, SEQ], BF16, "hT")
                    views = [Tk(None, f"hv{i}") for i in range(NT)]
                    with ExitStack() as pa:
                        self.norm_T(pa, xin, NT, I[f"l{l}_norm"], views, hT.ap, nb=2)
                        S.barrier()
                    self.blend_tiles(lambda m: hTo[:, :, m * 128:(m + 1) * 128], lambda t: hT[:, :, t * 128:(t + 1) * 128],
                                     hTo, hT)
                    wb = [self.sb(ph, [128, KC, 512], BF16, "wp") for _ in range(2)]
                    gkv = self.sb(ph, [128, 512], F32, "gkv")
                    S.dma("sp", gkv[:], I["l3_kv_norm"].partition_broadcast(128), writes=[gkv])
                    tmpA = self.sb(ph, [128, 512], F32, "krA")
                    tmpB = self.sb(ph, [128, 512], F32, "krB")

                    def post_kr(jb, t0, t1, pb):
                        self.rope_block(pb, 64, t0, t1, tmpA, tmpB, krT[0:64, t0:t1], krT)
                    pkv = latent_post(ph, gkv)(ckvT, ckv_views)
                    jobs = [{"mode": "T", "c0": o[1], "nc": 512, "post": pkv, "flush": pkv.flush},
                            {"mode": "F", "c0": o[2], "nc": 64, "post": post_kr}]
                    self.gemm(ph, hT.ap, views, SEQ, KC, I[f"l{l}_w_in"], jobs, wbufs=wb)
                    S.barrier()
                with ExitStack() as ph:
                    wb = [self.sb(ph, [128, KC, 512], BF16, "wp") for _ in range(3)]
                    gq = self.sb(ph, [128, 512], F32, "gq")
                    S.dma("sp", gq[:], I["l3_q_norm"].partition_broadcast(128), writes=[gq])
                    sh = self.shared(ph)
                    pcq = latent_post(ph, gq)(cqT, cq_views)
                    jobs = [{"mode": "T", "c0": o[0], "nc": 512, "post": pcq, "flush": pcq.flush}]
                    pqm = self.post_F_store(ph, lambda jb: R["qmT"][jb["h"]][:, 0:HS], "copy", sh=sh)
                    for hh in range(4):
                        jobs.append({"mode": "F", "c0": o[3] + hh * 128, "nc": 128, "h": hh, "post": pqm, "ntok": HS})
                    pz = self.post_F_store(ph, lambda jb: R["zT"][jb["h"]][:, 0:HS], "silu", sh=sh)
                    for cc in range(20):
                        jobs.append({"mode": "F", "c0": o[4] + cc * 128, "nc": 128, "h": cc, "post": pz, "ntok": HS})
                    self.gemm(ph, hTo.ap, oviews, HS, KC, I[f"l{l}_w_in"], jobs, wbufs=wb)
                    S.barrier()

            with ExitStack() as ph:
                jobs = []
                cos_o = self.sb(ph, [64, HS], F32, "cos64o")
                sin_o = self.sb(ph, [64, HS], F32, "sin64o")
                self.blend_tiles(lambda m: cos_o[0:64, m * 128:(m + 1) * 128],
                                 lambda t: C["cos64"][0:64, t * 128:(t + 1) * 128], cos_o, C["cos64"])
                self.blend_tiles(lambda m: sin_o[0:64, m * 128:(m + 1) * 128],
                                 lambda t: C["sin64"][0:64, t * 128:(t + 1) * 128], sin_o, C["sin64"])
                sh = self.shared(ph, rope=True)
                pq = self.post_F_store(ph, lambda jb: R["qT"][jb["h"]][:, 0:HS], "copy", sh=sh)
                pqr = self.post_F_store(ph, lambda jb: R["q64T"][jb["h"]][:, 0:HS], "rope64", npart=64, sh=sh,
                                        tabs=(cos_o, sin_o))
                for hh in range(16):
                    jobs.append({"mode": "F", "c0": hh * 192, "nc": 128, "h": hh, "post": pq, "ntok": HS})
                    jobs.append({"mode": "F", "c0": hh * 192 + 128, "nc": 64, "h": hh, "post": pqr, "ntok": HS})
                self.gemm(ph, cqT.ap, cq_views, HS, 4, I["l3_w_uq"], jobs)
                S.barrier()
            with ExitStack() as ph:
                jobs = []
                sh = self.shared(ph)
                pk = self.post_F_store(ph, lambda jb: R["kT"][jb["h"]], "copy", sh=sh)

                def dstT(jb, i, stgt):
                    return (R["v"][jb["h"], i * 128:(i + 1) * 128, :], stgt[:, 0:128])
                def dstT4(jb, i, stgt):
                    h0 = jb["h0"]
                    return (R["v"][h0:h0 + 4, i * 128:(i + 1) * 128, :].rearrange("h p d -> p h d"),
                            stgt[:, 0:512].rearrange("p (h d) -> p h d", h=4))
                pv = self.post_T_store(ph, dstT4)
                wb2 = [self.sb(ph, [128, 4, 1024], BF16, "wp2") for _ in range(3)]
                for h0 in range(0, 16, 4):
                    pan = (h0 * 256, h0 * 256 + 1024)
                    for hh in range(h0, h0 + 4):
                        jobs.append({"mode": "F", "c0": hh * 256, "nc": 128, "h": hh, "post": pk, "panel": pan})
                    jobs.append({"mode": "T", "c0": h0 * 256, "nc": 512, "h0": h0, "post": pv, "panel": pan,
                                 "rsel": lambda wp, kc: wp[:, kc, 0:1024].rearrange("p (h t d) -> p h t d", h=4, t=2)[:, :, 1, :],
                                 "osel": lambda pb: pb[:, 0:512].rearrange("p (h d) -> p h d", h=4)})
                self.gemm(ph, ckvT.ap, ckv_views, SEQ, 4, I["l3_w_ukv"], jobs, wbufs=wb2)
                S.barrier()

            gT = self.sb(lay, [128, 20, HS], BF16, "gT")
            gviews = [Tk(None, f"gv{i}") for i in range(20)]
            with ExitStack() as ph:
                W = self.attn_work(ph, skew=4)
                heads = []
                for hh in range(16):
                    heads.append([{"q": R["qT"][hh], "k": R["kT"][hh], "v": R["v"][hh], "q64": R["q64T"][hh],
                                   "blocks_fn": lambda c: self.own_blocks(c)}])
                self.attn_heads(ph, W, heads + self.mem_head_specs(l), 192 ** -0.5, 0, load_q64=krT, nq=HS,
                                gT=gT, gviews=gviews)
                S.barrier()
            nchunk = 20
            with ExitStack() as ph:
                xe = [self.sb(ph, [128, 512], F32, "xe") for _ in range(3)]
                xo_ = [self.sb(ph, [128, 512], F32, "xo") for _ in range(3)]
                wbufs = [self.sb(ph, [128, nchunk, 512], BF16, "wo") for _ in range(2)]
                wsrc = I[f"l{l}_w_out"].rearrange("(kc p) c -> p kc c", p=128)
                sel = C["sel"]
                n = 0
                for pc in range(4):
                    wp = wbufs[pc % 2]
                    S.dma("pool", wp[:], wsrc[:, :, pc * 512:(pc + 1) * 512], writes=[wp])
                    for m in range(NT // 2):
                        xa, xb_ = xe[n % 3], xo_[n % 3]
                        n += 1
                        S.dma("act", xa[:], xin[(2 * m) * 128:(2 * m + 1) * 128, pc * 512:(pc + 1) * 512], writes=[xa])
                        S.dma("act", xb_[:], xin[(2 * m + 1) * 128:(2 * m + 2) * 128, pc * 512:(pc + 1) * 512], writes=[xb_])
                        pb = self.bank("g")
                        for cc in range(nchunk):
                            S.op("pe", lambda h, pb=pb, cc=cc, m=m, wp=wp: h.matmul(
                                pb[:], gT[:, cc, m * 128:(m + 1) * 128], wp[:, cc, :],
                                start=(cc == 0), stop=(cc == nchunk - 1)), reads=[gviews[cc], wp], writes=[pb])
                        S.op("dve", lambda h, xa=xa: h.tensor_scalar(xa[:], xa[:], sel[:, 0:1], None, ALU.mult),
                             reads=[xa, sel], writes=[xa])
                        S.op("dve", lambda h, xa=xa, xb_=xb_: h.scalar_tensor_tensor(
                            xa[:], xb_[:], sel[:, 1:2], xa[:], ALU.mult, ALU.add), reads=[xa, xb_, sel], writes=[xa])
                        S.op("dve", lambda h, xa=xa, pb=pb: h.tensor_tensor(xa[:], xa[:], pb[:], ALU.add),
                             reads=[xa, pb], writes=[xa])
                        S.dma("sp", xout[m * 128:(m + 1) * 128, pc * 512:(pc + 1) * 512], xa[:], reads=[xa], dram_write=True)
                S.barrier()

    def rope_block(self, pb, npart, t0, t1, ta, tb, dst_ap, dst_tk):
        S, C = self.S, self.C
        hf = npart // 2
        cosT, sinT = (C["cos128"], C["sin128"]) if npart == 128 else (C["cos64"], C["sin64"])
        w = t1 - t0
        pp = slice(0, npart)
        S.op("dve", lambda h: h.tensor_tensor(ta[0:hf, 0:w], pb[hf:npart, 0:w], sinT[0:hf, t0:t1], ALU.mult),
             reads=[pb, sinT], writes=[ta])
        S.op("dve", lambda h: h.tensor_tensor(ta[hf:npart, 0:w], pb[0:hf, 0:w], sinT[hf:npart, t0:t1], ALU.mult),
             reads=[pb, sinT], writes=[ta])
        S.op("dve", lambda h: h.tensor_tensor(tb[pp, 0:w], pb[pp, 0:w], cosT[pp, t0:t1], ALU.mult),
             reads=[pb, cosT], writes=[tb])
        S.op("pool", lambda h: h.tensor_tensor(dst_ap, ta[pp, 0:w], tb[pp, 0:w], ALU.add),
             reads=[ta, tb], writes=[dst_tk])

    def layer0(self, xin, xout):
        S, R, I, C = self.S, self.R, self.I, self.C
        o = offs(A_SIZES)
        l = 0
        with ExitStack() as lay:
            logf = self.sb(lay, [128, NT, 16], F32, "logf")
            biasall = self.sb(lay, [128, NT // 2, 16, NT], F32, "biasall")
            lf_views = [Tk(None, f"lf{i}") for i in range(NT)]

            def make_jobs(ph):
                jobs = []
                sh = self.shared(ph)
                pq = self.post_F_store(ph, lambda jb: R["qT"][jb["h"]], "copy", sh=sh)
                pk = self.post_F_store(ph, lambda jb: R["kT"][jb["h"]], "copy", sh=sh)
                for hh in range(16):
                    jobs.append({"mode": "F", "c0": o[0] + hh * 128, "nc": 128, "h": hh, "post": pq})
                for hh in range(16):
                    jobs.append({"mode": "F", "c0": o[1] + hh * 128, "nc": 128, "h": hh, "post": pk})
                jobs += self.v_jobs(ph, o[2], 16)
                fb = self.sb(ph, [128, 16], F32, "fb")
                S.dma("sp", fb[:], I["l0_forget_bias"].partition_broadcast(128), writes=[fb])
                ft = self.sb(ph, [128, 16], F32, "ft")

                def post_f(jb, i, pb):
                    S.op("dve", lambda h: h.tensor_tensor(ft[:], pb[:, 0:16], fb[:], ALU.add), reads=[pb, fb], writes=[ft])
                    S.op("act", lambda h: h.activation(ft[:], ft[:], AF.Exp, scale=-1.0), reads=[ft], writes=[ft])
                    S.op("act", lambda h: h.activation(ft[:], ft[:], AF.Ln, bias=1.0, scale=1.0), reads=[ft], writes=[ft])
                    S.op("dve", lambda h: h.tensor_scalar(logf[:, i, :], ft[:], -1.0, None, ALU.mult),
                         reads=[ft], writes=[lf_views[i]])
                jobs.append({"mode": "T", "c0": o[3], "nc": 16, "post": post_f})
                jobs += self.std_jobs(ph, sh, o[4], o[5], 20)
                return jobs

            def extra(ph):
                tri = self.sb(ph, [128, 128], F32, "tri32")
                one = self.sb(ph, [128, 128], F32, "one32")
                S.dma("sp", tri[:], I["c_tri32"], writes=[tri])
                S.dma("sp", one[:], I["c_ones32"], writes=[one])
                cw = self.sb(ph, [128, NT, 16], F32, "cw")
                tot = self.sb(ph, [128, NT + 1, 16], F32, "tot")
                lfa = logf[:].rearrange("p i h -> p (i h)")
                p1, p2 = self.bank("y"), self.bank("y")
                S.op("pe", lambda h: h.matmul(p1[:, 0:256], tri[:], lfa, start=True, stop=True),
                     reads=[tri] + lf_views, writes=[p1])
                S.op("pe", lambda h: h.matmul(p2[:, 0:256], one[:], lfa, start=True, stop=True),
                     reads=[one] + lf_views, writes=[p2])
                S.op("dve", lambda h: h.memset(tot[:, 0, :], 0.0), writes=[tot])
                for i in range(NT):
                    S.op("dve", lambda h, i=i: h.tensor_tensor(tot[:, i + 1, :], tot[:, i, :], p2[:, i * 16:(i + 1) * 16], ALU.add),
                         reads=[tot, p2], writes=[tot])
                S.op("dve", lambda h: h.tensor_tensor(cw[:].rearrange("p i h -> p (i h)"), p1[:, 0:256],
                                                      tot[:, 0:NT, :].rearrange("p i h -> p (i h)"), ALU.add),
                     reads=[p1, tot], writes=[cw])
                for u in range(NT // 2):
                    S.op("dve", lambda h, u=u: h.tensor_tensor(
                        biasall[:, u, :, :], tot[:, 2 * u + 1, :].unsqueeze(2).to_broadcast([128, 16, NT]),
                        cw[:].rearrange("p j h -> p h j"), ALU.subtract), reads=[tot, cw], writes=[biasall])
                    S.op("dve", lambda h, u=u: h.tensor_scalar(
                        biasall[:, u, :, :], biasall[:, u, :, :], 60.0, None, ALU.min), reads=[biasall], writes=[biasall])

            self.project(l, xin, make_jobs, extra)
            gT = self.sb(lay, [128, 20, SEQ], BF16, "gT")
            gviews = [Tk(None, f"gv{i}") for i in range(20)]
            wo0 = self.sb(lay, [128, 20, 512], BF16, "wo0")
            S.dma("pool", wo0[:], I[f"l{l}_w_out"].rearrange("(kc p) c -> p kc c", p=128)[:, :, 0:512], writes=[wo0])

            with ExitStack() as ph:
                W = self.attn_work(ph, skew=4)
                heads = []
                for hh in range(16):
                    bf = (lambda hh: (lambda i, j: (biasall[:, i, hh, j:j + 1], biasall)))(hh)
                    heads.append([{"q": R["qT"][hh], "k": R["kT"][hh], "v": R["v"][hh],
                                   "blocks_fn": (lambda bf: (lambda c: self.causal_blocks(c, biasf=bf)))(bf)}])
                self.attn_heads(ph, W, heads + self.mem_head_specs(l), 128 ** -0.5, 0, gT=gT, gviews=gviews)
                S.barrier()
            self.out_proj(l, 20, xin, xout, gT=gT, gviews=gviews, wpre=wo0)


    def dil_blocks(self, c, g):
        maxd, diag, mid, edge = ((1, "tri", None, "low"), (4, "m4tri", "m4", "m4low"), (NT, "m16tri", "m16", "m16"))[g]
        M = self.C["masks"]
        blocks = []
        for j in range(max(0, 4 * c - maxd), 4 * c + 4):
            a = max(j, 4 * c) - 4 * c
            b = min(j + maxd, 4 * c + 3) - 4 * c + 1
            masks = []
            for qt in range(a, b):
                dlt = 4 * c + qt - j
                nm = diag if dlt == 0 else (edge if dlt == maxd else mid)
                masks.append((self.mask(nm), M, nm))
            blocks.append((j, a, b, masks, None))
        return blocks

    def layer2(self, xin, xout):
        S, R, I, C = self.S, self.R, self.I, self.C
        o = offs(C_SIZES)
        l = 2

        def make_jobs(ph):
            jobs = []
            sh = self.shared(ph, rope=True)
            pq = self.post_F_store(ph, lambda jb: R["qT"][jb["h"]], "rope128", sh=sh)
            pk = self.post_F_store(ph, lambda jb: R["kT"][jb["h"]], "rope128", sh=sh)
            for hh in range(18):
                jobs.append({"mode": "F", "c0": o[0] + hh * 128, "nc": 128, "h": hh, "post": pq})
            for hh in range(18):
                jobs.append({"mode": "F", "c0": o[1] + hh * 128, "nc": 128, "h": hh, "post": pk})
            jobs += self.v_jobs(ph, o[2], 18)
            jobs += self.std_jobs(ph, sh, o[3], o[4], 10)
            return jobs

        self.project(l, xin, make_jobs)
        lay = ExitStack()
        gT = self.sb(lay, [128, 10, SEQ], BF16, "gT")
        gviews = [Tk(None, f"gv{i}") for i in range(10)]
        wo0 = self.sb(lay, [128, 10, 512], BF16, "wo0")
        S.dma("pool", wo0[:], I[f"l{l}_w_out"].rearrange("(kc p) c -> p kc c", p=128)[:, :, 0:512], writes=[wo0])
        with ExitStack() as ph:
            W = self.attn_work(ph, recip_act=True, skew=4)
            heads = []
            for hh in range(6):
                subs = []
                for g in range(3):
                    n = g * 6 + hh
                    subs.append({"q": R["qT"][n], "k": R["kT"][n], "v": R["v"][n],
                                 "blocks_fn": (lambda g: (lambda c: self.dil_blocks(c, g)))(g)})
                heads.append(subs)
            self.attn_heads(ph, W, heads + self.mem_head_specs(l), 128 ** -0.5, 0, gT=gT, gviews=gviews)
            S.barrier()
        self.out_proj(l, 10, xin, xout, gT=gT, gviews=gviews, wpre=wo0)
        lay.close()

    def layer1(self, xin, xout):
        S, R, I, C = self.S, self.R, self.I, self.C
        o = offs(B_SIZES)
        l = 1
        BIG = 1.0e30
        with ExitStack() as lay:
            kiT = self.sb(lay, [64, SEQ], BF16, "kiT")
            widx = self.sb(lay, [128, NT, 16], F32, "widx")
            wi_views = [Tk(None, f"wi{i}") for i in range(NT)]

            def make_jobs(ph):
                jobs = []
                sh = self.shared(ph, rope=True)
                pq = self.post_F_store(ph, lambda jb: R["qT"][jb["h"]], "rope128", sh=sh)
                pk = self.post_F_store(ph, lambda jb: R["kT"][0], "rope128", sh=sh)
                for hh in range(16):
                    jobs.append({"mode": "F", "c0": o[0] + hh * 128, "nc": 128, "h": hh, "post": pq})
                jobs.append({"mode": "F", "c0": o[1], "nc": 128, "h": 0, "post": pk})
                jobs += self.v_jobs(ph, o[2], 1)
                pqi = self.post_F_store(ph, lambda jb: R["q64T"][jb["h"]], "rope64", npart=64, sh=sh)
                for hh in range(16):
                    jobs.append({"mode": "F", "c0": o[3] + hh * 64, "nc": 64, "h": hh, "post": pqi})
                tA, tB = sh["tmpA"][0], sh["tmpB"][0]

                def post_ki(jb, t0, t1, pb):
                    self.rope_block(pb, 64, t0, t1, tA, tB, kiT[0:64, t0:t1], kiT)
                jobs.append({"mode": "F", "c0": o[4], "nc": 64, "post": post_ki})

                def post_w(jb, i, pb):
                    S.op("act", lambda h: h.copy(widx[:, i, :], pb[:, 0:16]), reads=[pb], writes=[wi_views[i]])
                jobs.append({"mode": "T", "c0": o[5], "nc": 16, "post": post_w})
                jobs += self.std_jobs(ph, sh, o[6], o[7], 20)
                return jobs

            self.project(l, xin, make_jobs)

            with ExitStack() as ph:
                aw = self.sb(ph, [128, NT, 16], F32, "aw")
                sg = self.sb(ph, [128, NT, 16], F32, "sg")
                pm = self.sb(ph, [128, 128], F32, "pm")
                S.op("act", lambda h: h.activation(aw[:], widx[:], AF.Abs), reads=wi_views, writes=[aw])
                S.op("dve", lambda h: h.tensor_scalar(sg[:], widx[:], 0.0, 2.0, ALU.is_ge, ALU.mult), reads=wi_views, writes=[sg])
                S.op("dve", lambda h: h.tensor_scalar(sg[:], sg[:], -1.0, None, ALU.add), reads=[sg], writes=[sg])
                S.op("dve", lambda h: h.tensor_scalar(pm[:], self.mask("low"), 2.0, -1.0, ALU.mult, ALU.add),
                     reads=[C["masks"]], writes=[pm])
                S.op("dve", lambda h: h.tensor_scalar(pm[:], pm[:], BIG, None, ALU.mult), reads=[pm], writes=[pm])
                G = 4
                NIT = 24
                qis = [self.sb(ph, [64, 16, 128], BF16, "qi") for _ in range(3)]
                accs = [self.sb(ph, [128, SEQ], F32, "acc") for _ in range(2 * G)]
                junks = [self.sb(ph, [128, SEQ], BF16, "sjunk") for _ in range(G)]
                rbuf = [self.sb(ph, [128, 512], F32, "rb") for _ in range(6)]
                sts = [self.sb(ph, [128, 8], F32, "bst") for _ in range(2 * G)]
                mqs = [self.sb(ph, [128, SEQ], BF16, "mq") for _ in range(2)]
                mts = [self.sb(ph, [128, NT, 128], BF16, "mts") for _ in range(2)]
                cnt = {"nr": 0, "nm": 0}
                groups = [list(range(g0, min(NT, g0 + G))) for g0 in range(0, NT, G)]

                def score_units(tiles):
                    units = []
                    for i in tiles:
                        qi, acc, stt = qis[i % 3], accs[i % (2 * G)], sts[i % (2 * G)]
                        Wd = (i + 1) * 128
                        first = [True]
                        for s0 in range(0, Wd, 512):
                            w = min(512, Wd - s0)
                            for hh in range(16):
                                def unit(i=i, qi=qi, acc=acc, s0=s0, w=w, hh=hh, ld=(s0 == 0 and hh == 0)):
                                    if ld:
                                        S.dma("sp", qi[:], R["q64T"][:, :, i * 128:(i + 1) * 128].rearrange("h d t -> d h t"), writes=[qi])
                                    pb = self.bank("g")
                                    S.op("pe", lambda h: h.matmul(
                                        pb[:, 0:w], qi[:, hh, :], kiT[0:64, s0:s0 + w], start=True, stop=True),
                                        reads=[qi, kiT], writes=[pb])
                                    rb = rbuf[cnt["nr"] % 6]
                                    cnt["nr"] += 1
                                    if hh % 4 != 3:
                                        S.op("act", lambda h: h.activation(
                                            rb[:, 0:w], pb[:, 0:w], AF.Relu, scale=aw[:, i, hh:hh + 1]), reads=[pb, aw], writes=[rb])
                                    else:
                                        S.op("dve", lambda h: h.tensor_scalar(
                                            rb[:, 0:w], pb[:, 0:w], aw[:, i, hh:hh + 1], 0.0, ALU.mult, ALU.max),
                                            reads=[pb, aw], writes=[rb])
                                    if hh == 0:
                                        S.op("dve", lambda h: h.tensor_scalar(
                                            acc[:, s0:s0 + w], rb[:, 0:w], sg[:, i, hh:hh + 1], None, ALU.mult),
                                            reads=[rb, sg], writes=[acc])
                                    else:
                                        S.op("dve", lambda h: h.scalar_tensor_tensor(
                                            acc[:, s0:s0 + w], rb[:, 0:w], sg[:, i, hh:hh + 1], acc[:, s0:s0 + w], ALU.mult, ALU.add),
                                            reads=[rb, sg, acc], writes=[acc])
                                units.append(unit)

                        def prep(i=i, acc=acc, stt=stt, Wd=Wd):
                            if i >= 2:
                                S.op("dve", lambda h: h.tensor_reduce(stt[:, 1:2], acc[:, 0:Wd], AX.X, ALU.max), reads=[acc], writes=[stt])
                                S.op("dve", lambda h: h.tensor_reduce(stt[:, 5:6], acc[:, 0:Wd], AX.X, ALU.min), reads=[acc], writes=[stt])
                                S.op("dve", lambda h: h.tensor_tensor(stt[:, 1:2], stt[:, 1:2], stt[:, 5:6], ALU.subtract), reads=[stt], writes=[stt])
                                S.op("dve", lambda h: h.tensor_scalar(stt[:, 0:1], stt[:, 5:6], -1.0, None, ALU.mult), reads=[stt], writes=[stt])
                            S.op("dve", lambda h: h.tensor_tensor(
                                acc[:, i * 128:(i + 1) * 128], acc[:, i * 128:(i + 1) * 128], pm[:], ALU.min), reads=[acc, pm], writes=[acc])
                        units.append(prep)
                    return units

                def bisect_steps(tiles):
                    steps = []
                    act_tiles = [i for i in tiles if i >= 2]
                    for k in range(NIT):
                        def issue(k=k):
                            step = 2.0 ** -(k + 1)
                            for i in act_tiles:
                                stt = sts[i % (2 * G)]
                                S.op("dve", lambda h, stt=stt: h.scalar_tensor_tensor(
                                    stt[:, 2:3], stt[:, 1:2], -step, stt[:, 0:1], ALU.mult, ALU.add), reads=[stt], writes=[stt])
                            for i in act_tiles:
                                acc, stt, jk = accs[i % (2 * G)], sts[i % (2 * G)], junks[i % G]
                                Wd = (i + 1) * 128
                                S.op("act", lambda h, acc=acc, stt=stt, jk=jk, Wd=Wd: h.activation(
                                    jk[:, 0:Wd], acc[:, 0:Wd], AF.Sign, bias=stt[:, 2:3], scale=1.0, accum_out=stt[:, 3:4]),
                                    reads=[acc, stt], writes=[jk, stt])

                        def update(k=k):
                            step = 2.0 ** -(k + 1)
                            for i in act_tiles:
                                stt = sts[i % (2 * G)]
                                Wd = (i + 1) * 128
                                S.op("dve", lambda h, stt=stt, Wd=Wd: h.tensor_scalar(
                                    stt[:, 4:5], stt[:, 3:4], float(512 - Wd), -step, ALU.is_ge, ALU.mult), reads=[stt], writes=[stt])
                                S.op("dve", lambda h, stt=stt: h.scalar_tensor_tensor(
                                    stt[:, 0:1], stt[:, 4:5], stt[:, 1:2], stt[:, 0:1], ALU.mult, ALU.add), reads=[stt], writes=[stt])
                        steps.append((issue, update))
                    return steps

                def emit_masks(tiles):
                    for i in tiles:
                        acc, stt = accs[i % (2 * G)], sts[i % (2 * G)]
                        mq, mt = mqs[cnt["nm"] % 2], mts[cnt["nm"] % 2]
                        cnt["nm"] += 1
                        Wd = (i + 1) * 128
                        if i >= 2:
                            S.op("dve", lambda h, stt=stt: h.tensor_scalar(
                                stt[:, 5:6], stt[:, 0:1], -1.0, None, ALU.mult), reads=[stt], writes=[stt])
                            S.op("dve", lambda h, acc=acc, mq=mq, Wd=Wd, stt=stt: h.tensor_scalar(
                                mq[:, 0:Wd], acc[:, 0:Wd], stt[:, 5:6], None, ALU.is_ge), reads=[acc, stt], writes=[mq])
                        else:
                            S.op("dve", lambda h, acc=acc, mq=mq, Wd=Wd: h.tensor_scalar(
                                mq[:, 0:Wd], acc[:, 0:Wd], -0.5 * BIG, None, ALU.is_gt), reads=[acc], writes=[mq])
                        for j0 in range(0, i + 1, 8):
                            nj = min(8, i + 1 - j0)
                            tb = self.bank("t")
                            tbv = tb.ap.bitcast(BF16)
                            for k in range(nj):
                                S.op("pe", lambda h, tbv=tbv, k=k, j0=j0, mq=mq: h.transpose(
                                    tbv[:, k * 128:(k + 1) * 128], mq[:, (j0 + k) * 128:(j0 + k + 1) * 128], C["ident"][:]),
                                    reads=[mq, C["ident"]], writes=[tb])
                            S.op("act", lambda h, tbv=tbv, nj=nj, j0=j0, mt=mt: h.copy(
                                mt[:, j0:j0 + nj, :], tbv[:, 0:nj * 128].rearrange("p (k t) -> p k t", k=nj)),
                                reads=[tb], writes=[mt])
                        jmax = 4 * (i // 4) + 3
                        if jmax > i:
                            S.op("pool", lambda h, mt=mt, i=i, jmax=jmax: h.memset(mt[:, i + 1:jmax + 1, :], 0.0), writes=[mt])
                        S.dma("sp", R["mT"][0:jmax + 1, :, i * 128:(i + 1) * 128].rearrange("j p q -> p j q"),
                              mt[:, 0:jmax + 1, :], reads=[mt], dram_write=True)

                for u in score_units(groups[0]):
                    u()
                for gi, tiles in enumerate(groups):
                    steps = bisect_steps(tiles)
                    nxt = score_units(groups[gi + 1]) if gi + 1 < len(groups) else []
                    per = (len(nxt) + NIT - 1) // NIT if nxt else 0
                    ui = 0
                    for k, (issue, update) in enumerate(steps):
                        issue()
                        for _ in range(per):
                            if ui < len(nxt):
                                nxt[ui]()
                                ui += 1
                        update()
                    while ui < len(nxt):
                        nxt[ui]()
                        ui += 1
                    emit_masks(tiles)
                S.barrier()

            gT = self.sb(lay, [128, 20, SEQ], BF16, "gT")
            gviews = [Tk(None, f"gv{i}") for i in range(20)]
            wo0 = self.sb(lay, [128, 20, 512], BF16, "wo0")
            S.dma("pool", wo0[:], I[f"l{l}_w_out"].rearrange("(kc p) c -> p kc c", p=128)[:, :, 0:512], writes=[wo0])
            with ExitStack() as ph:
                W = self.attn_work(ph, recip_act=True, skew=4)
                kT = self.sb(ph, [128, SEQ], BF16, "kTd")
                vv = self.sb(ph, [128, NT, 128], BF16, "vd")
                S.dma("sp", kT[:], R["kT"][0], writes=[kT])
                S.dma("sp", vv[:], R["v"][0].rearrange("(j p) d -> p j d", p=128), writes=[vv])
                mtc = [self.sb(ph, [128, 12, 512], BF16, "mtc"), self.sb(ph, [128, NT, 512], BF16, "mtc")]
                qcs = [self.sb(ph, [128, 512], BF16, "qc") for _ in range(3)]
                zcs = [self.sb(ph, [128, 512], BF16, "zc") for _ in range(3)]
                gcs = [self.sb(ph, [128, 512], BF16, "gc") for _ in range(3)]
                n = 0
                for c in range(NT // 4):
                    mc = mtc[c % 2]
                    nj = 4 * c + 4
                    S.dma("sp", mc[:, 0:nj, :], R["mT"][0:nj, :, c * 512:(c + 1) * 512].rearrange("j p q -> p j q"), writes=[mc])
                    for hh in range(16):
                        qc, zc, gc = qcs[n % 3], zcs[n % 3], gcs[n % 3]
                        n += 1
                        S.dma("sp", qc[:], R["qT"][hh][:, c * 512:(c + 1) * 512], writes=[qc])
                        S.dma("sp", zc[:], R["zT"][hh][:, c * 512:(c + 1) * 512], writes=[zc])
                        am = (lambda mc, c: (lambda j, iq: (mc[:, j, (iq - 4 * c) * 128:(iq - 4 * c + 1) * 128], mc)))(mc, c)
                        bm = (lambda mc: (lambda j, a, b: (mc[:, j, a * 128:b * 128], mc)))(mc)
                        sub = {"parts": [(kT.ap, kT, qc.ap, qc)], "v_ap": vv.ap, "v_tk": vv,
                               "blocks": self.causal_blocks(c, allmask=am), "qcol0": 0, "blockmask": bm}
                        self.attn_unit(W, 4 * c, 4, [sub], 128 ** -0.5, zc[:], zc,
                                       gT[:, hh, c * 512:(c + 1) * 512], gviews[hh], 0)
                self.attn_flush(W)
                S.barrier()
            with ExitStack() as ph:
                W = self.attn_work(ph, recip_act=True)
                self.mem_heads(ph, W, 16, l, gT=gT, gviews=gviews)
                S.barrier()
            self.out_proj(l, 20, xin, xout, gT=gT, gviews=gviews, wpre=wo0)


_CACHE = {}


def run(inputs, layers=(0, 1, 2, 3), final=True, cores=8, own3=False, r_override=None):
    import ml_dtypes
    key = (tuple(layers), final, own3)
    if key not in _CACHE:
        _CACHE[key] = Builder(layers, final, own3=own3).build()
    nc = _CACHE[key]
    consts = host_consts()
    shared = dict(consts)
    for k, v in inputs.items():
        if k in ("x", "mem", "positions"):
            continue
        a = np.ascontiguousarray(np.asarray(v))
        if a.ndim == 1:
            a = a.reshape(1, -1)
        shared[k] = a
    kp = np.arange(128)[:, None]
    qf = np.arange(128)[None, :]
    tri = (qf >= kp).astype(np.float32)
    in_maps = []
    for c in range(cores):
        b = c // 2 if cores == 8 else c
        r = c % 2 if cores == 8 else 0
        if r_override is not None:
            r = r_override
        m = dict(shared)
        m["x"] = np.ascontiguousarray(np.asarray(inputs["x"])[b])
        m["mem"] = np.ascontiguousarray(np.asarray(inputs["mem"])[b])
        m["pos"] = np.ascontiguousarray(np.asarray(inputs["positions"])[b].reshape(1, -1).astype(np.int32))
        sel = np.zeros((128, 2), np.float32)
        sel[:, r] = 1.0
        m["c_sel"] = sel
        own = np.concatenate([tri if r == 0 else np.ones_like(tri), np.zeros_like(tri) if r == 0 else tri], axis=1)
        m["c_own"] = own.astype(ml_dtypes.bfloat16)
        in_maps.append(m)
    res = run_bass_kernel_spmd(nc, in_maps, core_ids=list(range(cores)))
    return res


def kernel(**inputs):
    res = run(inputs, own3=True)
    B = 4
    out = np.empty((B, NT // 2, 2, 128, D), np.float32)
    for b in range(B):
        for r in range(2):
            out[b, :, r] = res.results[2 * b + r]["out"].reshape(NT // 2, 128, D)
    return out.reshape(B, SEQ, D)
```
